# Optimizing a Trainium2 kernel written in Bass

```python
import jax, jax.numpy as jnp
from jax import lax
import numpy as np

D_MODEL = 1024
BATCH = 32
SEQ = 2048
DEPTH = 4

N_FOURIER_GROUPS = 8
D_FOURIER = D_MODEL // 2
FOURIER_GROUP_DIM = D_FOURIER // N_FOURIER_GROUPS
CHUNK = 128
N_SGU_HEADS = 8
D_SGU = D_MODEL
SGU_HEAD_DIM = D_SGU // N_SGU_HEADS
IN_COLS = D_FOURIER + 2 * D_SGU + 2 * D_MODEL
D_FF = 4 * D_MODEL
EPS = 1e-6

kernel_name = "hybrid_fnet_gmlp_gated_encoder"


def rms_norm(x, g):
    xf = x.astype(jnp.float32)
    y = xf * lax.rsqrt(jnp.mean(xf * xf, axis=-1, keepdims=True) + EPS)
    return (y * g.astype(jnp.float32)).astype(x.dtype)


def layer_norm(x, g, b):
    xf = x.astype(jnp.float32)
    mu = jnp.mean(xf, axis=-1, keepdims=True)
    xc = xf - mu
    y = xc * lax.rsqrt(jnp.mean(xc * xc, axis=-1, keepdims=True) + EPS)
    return (y * g.astype(jnp.float32) + b.astype(jnp.float32)).astype(x.dtype)


def fourier_mixer(a):
    bsz, seq, _ = a.shape
    ag = a.reshape(bsz, seq, N_FOURIER_GROUPS, FOURIER_GROUP_DIM).astype(jnp.float32)
    y = jnp.fft.fft2(ag, axes=(1, 3), norm="ortho").real
    return y.reshape(bsz, seq, D_FOURIER).astype(a.dtype)


def spatial_gating(u, v, ln_g, ln_b, w_s, b_s):
    bsz, seq, _ = v.shape
    n_chunks = seq // CHUNK
    vn = layer_norm(v, ln_g, ln_b).reshape(bsz, n_chunks, CHUNK, N_SGU_HEADS, SGU_HEAD_DIM)
    mixed = jnp.einsum('hqp,bnphd->bnqhd', w_s, vn) + b_s.T[None, None, :, :, None]
    return u * mixed.reshape(bsz, seq, D_SGU)


def setup_inputs(seed: int = 0) -> dict:
    key = jax.random.key(seed)
    ks = jax.random.split(key, 16)
    f32 = jnp.float32
    nrm = lambda k, shape, scale: jax.random.normal(k, shape, f32) * scale
    return {
        "x": nrm(ks[0], (BATCH, SEQ, D_MODEL), 1.0),
        "g_mix": 1.0 + nrm(ks[1], (DEPTH, D_MODEL), 0.02),
        "w_in": nrm(ks[2], (DEPTH, D_MODEL, IN_COLS), D_MODEL ** -0.5),
        "w_a": nrm(ks[3], (DEPTH, D_FOURIER, D_MODEL), D_FOURIER ** -0.5),
        "ln_v_g": 1.0 + nrm(ks[4], (DEPTH, D_SGU), 0.02),
        "ln_v_b": nrm(ks[5], (DEPTH, D_SGU), 0.02),
        "w_s": nrm(ks[6], (DEPTH, N_SGU_HEADS, CHUNK, CHUNK), CHUNK ** -0.5),
        "b_s": 1.0 + nrm(ks[7], (DEPTH, N_SGU_HEADS, CHUNK), 0.1),
        "w_b": nrm(ks[8], (DEPTH, D_SGU, D_MODEL), D_SGU ** -0.5),
        "w_out": nrm(ks[9], (DEPTH, D_MODEL, D_MODEL), D_MODEL ** -0.5),
        "g_mlp": 1.0 + nrm(ks[10], (DEPTH, D_MODEL), 0.02),
        "w_up": nrm(ks[11], (DEPTH, D_MODEL, D_FF), D_MODEL ** -0.5),
        "w_down": nrm(ks[12], (DEPTH, D_FF, D_MODEL), D_FF ** -0.5),
        "g_final": 1.0 + nrm(ks[13], (D_MODEL,), 0.02),
    }


def reference(x, g_mix, w_in, w_a, ln_v_g, ln_v_b, w_s, b_s, w_b, w_out,
              g_mlp, w_up, w_down, g_final):
    c_a = D_FOURIER
    c_u = c_a + D_SGU
    c_v = c_u + D_SGU
    c_ga = c_v + D_MODEL
    for l in range(DEPTH):
        h = rms_norm(x, g_mix[l])
        z = jnp.einsum('bsd,dc->bsc', h, w_in[l])
        a_in = z[..., :c_a]
        uv = jax.nn.gelu(z[..., c_a:c_v])
        u, v = uv[..., :D_SGU], uv[..., D_SGU:]
        gate_a = jax.nn.sigmoid(z[..., c_v:c_ga])
        gate_b = jax.nn.sigmoid(z[..., c_ga:])
        y_a = jnp.einsum('bsc,cd->bsd', fourier_mixer(a_in), w_a[l])
        y_b = jnp.einsum('bsc,cd->bsd', spatial_gating(u, v, ln_v_g[l], ln_v_b[l], w_s[l], b_s[l]), w_b[l])
        merged = gate_a * y_a + gate_b * y_b
        x = x + jnp.einsum('bsd,de->bse', merged, w_out[l])
        h2 = rms_norm(x, g_mlp[l])
        f = jnp.square(jax.nn.relu(jnp.einsum('bsd,df->bsf', h2, w_up[l])))
        x = x + jnp.einsum('bsf,fd->bsd', f, w_down[l])
    return rms_norm(x, g_final)
```

```python
import math
from contextlib import ExitStack

import numpy as np
import ml_dtypes

import concourse.bass as bass
import concourse.mybir as mybir
from concourse.bass_utils import run_bass_kernel_spmd

F32 = mybir.dt.float32
BF16 = mybir.dt.bfloat16
AF = mybir.ActivationFunctionType
ALU = mybir.AluOpType

N_CORES = 8
DEPTH = 4
D = 1024
S = 2048
NSEQ = 4
NT = 4
TW = 512
RING_SLOTS = 5
RING_ELEMS = 2048
EPS = 1e-6
NORM = 1.0 / math.sqrt(2048.0 * 64.0)
TK = 1028

MODE = "fused"


class Buf:
    __slots__ = ("w", "r")

    def __init__(self):
        self.w = None
        self.r = {}


class Eng:
    def __init__(self, handle, key, is_pe=False):
        self.h = handle
        self.key = key
        self.cnt = 0
        self.seen = {}
        self.is_pe = is_pe


class Tracker:
    def __init__(self, nc, es):
        self.nc = nc
        self.es = es
        self.sems = {}
        self.dcnt = {}

    def sem(self, key):
        if key not in self.sems:
            self.sems[key] = self.es.enter_context(self.nc.semaphore(key))
            self.dcnt[key] = 0
        return self.sems[key]

    @staticmethod
    def _collect(reads, writes):
        deps = {}
        for b in reads:
            if b.w is not None:
                k, v = b.w
                if deps.get(k, 0) < v:
                    deps[k] = v
        for b in writes:
            if b.w is not None:
                k, v = b.w
                if deps.get(k, 0) < v:
                    deps[k] = v
            for k, v in b.r.items():
                if deps.get(k, 0) < v:
                    deps[k] = v
        return deps

    def _wait(self, E, deps):
        for k, v in deps.items():
            if E.is_pe and k == E.key:
                continue
            if E.seen.get(k, 0) >= v:
                continue
            E.h.wait_ge(self.sems[k], v)
            E.seen[k] = v

    @staticmethod
    def _mark(tag, reads, writes):
        k, v = tag
        for b in reads:
            if b.r.get(k, 0) < v:
                b.r[k] = v
        for b in writes:
            b.w = tag
            b.r = {}

    def op(self, E, fn, reads=(), writes=(), signal=True):
        self._wait(E, self._collect(reads, writes))
        inst = fn()
        if signal:
            inst.then_inc(self.sem(E.key), 1)
            E.cnt += 1
            tag = (E.key, E.cnt)
        else:
            tag = (E.key, E.cnt + 1)
        self._mark(tag, reads, writes)

    def dma(self, Q, semkey, fns, reads=(), writes=()):
        sem = self.sem(semkey)
        self._wait(Q, self._collect(reads, writes))
        for fn in fns:
            fn().then_inc(sem, 16)
            self.dcnt[semkey] += 16
        self._mark((semkey, self.dcnt[semkey]), reads, writes)


def handoff(src, dst):
    merged = {}
    for b in src:
        if b.w is not None:
            k, v = b.w
            if merged.get(k, 0) < v:
                merged[k] = v
        for k, v in b.r.items():
            if merged.get(k, 0) < v:
                merged[k] = v
    for d in dst:
        d.w = None
        d.r = dict(merged)


def build(L, final, nseq=NSEQ):
    nc = bass.Bass("TRN2", target_bir_lowering=False, dynamic_dma_scratch_size=4096)

    def dram(name, shape, dt=F32, kind="ExternalInput"):
        return nc.dram_tensor(name, shape, dt, kind=kind).ap()

    xT = dram("xT", [nseq, D, S])
    outT = dram("outT", [nseq, D, S], kind="ExternalOutput")
    w_in = dram("w_in", [L, D, 4608])
    w_a = dram("w_a", [L, 512, D])
    w_b = dram("w_b", [L, D, D])
    w_out = dram("w_out", [L, D, D])
    w_up = dram("w_up", [L, D, 4096])
    w_down = dram("w_down", [L, 4096, D])
    wst_d = dram("wst", [L, 128, 1024])
    lng_d = dram("lng", [L, 128, 1024])
    bsr_d = dram("bsr", [L, 128, 1024])
    NV = L * 24 + 8
    vecs_d = dram("vecs", [128, NV])
    cf_d = dram("cf", [1024, TK], BF16)
    sf_d = dram("sf", [1024, TK], BF16)
    bd_d = dram("bd", [128, 256], BF16)
    alt_d = dram("alt", [1, TK], BF16)

    with ExitStack() as es:
        def sb(name, shape, dt):
            return es.enter_context(nc.sbuf_tensor(name, shape, dt))

        X = sb("X", [128, 8, S], F32)
        H = sb("H", [128, 8, S], BF16)
        U = sb("U", [128, 8, S], BF16)
        MRGR = sb("MRGR", [128, 2 * 8 * TK], BF16)
        RING = sb("RING", [128, RING_SLOTS, RING_ELEMS], BF16)
        S32 = sb("S32", [128, 4, TW], F32)
        S16 = sb("S16", [128, 2, TW], BF16)
        VRAW = sb("VRAW", [128, 2, 1024], F32)
        VN = sb("VN", [128, 3, 1024], BF16)
        TT = sb("TT", [128, 1024], F32)
        WST = sb("WST", [128, 1024], BF16)
        LNG = sb("LNG", [128, 1024], F32)
        VECS = sb("VECS", [128, NV], F32)
        BD = sb("BD", [128, 256], BF16)
        ONES = sb("ONES", [128, 128], BF16)
        ALT = sb("ALT", [1, TK], BF16)
        PHT = sb("PHT", [1, 512], BF16)
        STATS = sb("STATS", [128, 2, 12], F32)
        MV = sb("MV", [128, 2, 2], F32)
        RS = sb("RS", [128, 2, 2], F32)
        ps = [es.enter_context(nc.psum_tensor(f"ps{i}", [128, TW], F32)) for i in range(8)]

        T = Tracker(nc, es)

        PE = Eng(nc.tensor, "pe", is_pe=True)
        ACT = Eng(nc.scalar, "act")
        DVE = Eng(nc.vector, "dve")
        SP = Eng(nc.sync, "sp")
        POOL = Eng(nc.gpsimd, "pool")
        for e in (PE, ACT, DVE):
            T.sem(e.key)

        XB = [[Buf() for _ in range(NT)] for _ in range(8)]
        HB = [[Buf() for _ in range(NT)] for _ in range(8)]
        UB = [[Buf() for _ in range(NT)] for _ in range(8)]
        MB = [[Buf() for _ in range(NT)] for _ in range(8)]
        AINB = [[Buf() for _ in range(NT)] for _ in range(4)]
        AEB = [Buf() for _ in range(4)]
        AOB = [Buf() for _ in range(4)]
        PQB = [Buf() for _ in range(8)]
        FAB = [Buf() for _ in range(4)]
        TABB = Buf()
        PHB = Buf()
        CONSTB = Buf()
        ONESB = Buf()
        VRB = [Buf(), Buf()]
        VNB = [Buf(), Buf(), Buf()]
        STB = [Buf(), Buf()]
        TTB = Buf()
        WSTB = Buf()
        LNGB = Buf()
        PB = [Buf() for _ in range(8)]
        S32B = [Buf() for _ in range(4)]
        S16B = [Buf() for _ in range(2)]
        RINGB = [Buf() for _ in range(RING_SLOTS)]
        allUB = [b for r in UB for b in r]
        allMB = [b for r in MB for b in r]
        allXB = [b for r in XB for b in r]

        st = {"bank": 0, "s32": 0, "s16": 0, "ring": 0}

        def bank():
            i = st["bank"]
            st["bank"] = (i + 1) % 8
            return ps[i], PB[i]

        def s32():
            i = st["s32"]
            st["s32"] = (i + 1) % 4
            return S32[:, i, :], S32B[i]

        def s16():
            i = st["s16"]
            st["s16"] = (i + 1) % 2
            return S16[:, i, :], S16B[i]

        def ring_load(src, kc, cols):
            i = st["ring"]
            st["ring"] = (i + 1) % RING_SLOTS
            view = RING[:, i, 0:kc * cols].rearrange("p (k c) -> p k c", k=kc)
            T.dma(POOL, f"ring{i}", [lambda: nc.gpsimd.dma_start(out=view, in_=src)],
                  writes=[RINGB[i]])
            return view, RINGB[i]

        def nsl(n):
            return slice(n * TW, (n + 1) * TW)

        def mm(out, lhsT, rhs, start, stop, reads, pbuf, signal):
            T.op(PE, lambda: nc.tensor.matmul(out, lhsT, rhs, start=start, stop=stop),
                 reads=reads, writes=[pbuf], signal=signal)

        def gemm(wl, act, evac, ns):
            bks = {n: bank() for n in ns}
            K = len(wl)
            for k in range(K):
                lh, lb = wl[k]
                for n in ns:
                    ra, rb = act(k, n)
                    pt, pbf = bks[n]
                    mm(pt[:], lh, ra, k == 0, k == K - 1, [lb, rb], pbf, k == K - 1)
            for n in ns:
                evac(n, *bks[n])

        def wview(w2d):
            return w2d.rearrange("(k p) m -> p k m", p=128)

        def AE(j):
            return U[:, 4 + j // 2, (j % 2) * 1024:(j % 2 + 1) * 1024]

        def AO(j):
            return U[:, 6 + j // 2, (j % 2) * 1024:(j % 2 + 1) * 1024]

        def PQ(t, which, j=None):
            base = (t % 2) * 1024 + which * 512
            if j is None:
                return U[:, t // 2, base:base + 512]
            return U[:, t // 2, base + j * 128:base + (j + 1) * 128]

        def CF(t):
            return MRGR[:, t * TK:(t + 1) * TK]

        def SF(t):
            return MRGR[:, 8 * TK + t * TK:8 * TK + (t + 1) * TK]

        def MRG(m, n):
            return MRGR[:, m * S + n * TW:m * S + (n + 1) * TW]

        T.dma(SP, "c0", [lambda: nc.sync.dma_start(out=VECS[:], in_=vecs_d),
                         lambda: nc.sync.dma_start(out=BD[:], in_=bd_d),
                         lambda: nc.sync.dma_start(out=ALT[:], in_=alt_d)],
              writes=[CONSTB])
        T.op(DVE, lambda: nc.vector.memset(ONES[:], 1.0), writes=[ONESB])

        def rmsnorm(gcol, inplace):
            for n in range(NT):
                pst, pbf = bank()
                for c in range(8):
                    sq, sqb = s16()
                    T.op(ACT, lambda: nc.scalar.activation(out=sq, in_=X[:, c, nsl(n)], func=AF.Square),
                         reads=[XB[c][n]], writes=[sqb])
                    mm(pst[:], ONES[:], sq, c == 0, c == 7, [sqb, ONESB], pbf, True)
                r, rb = s32()
                T.op(ACT, lambda: nc.scalar.activation(out=r, in_=pst[:], func=AF.Ln,
                                                       scale=1.0 / D, bias=EPS),
                     reads=[pbf], writes=[rb])
                T.op(ACT, lambda: nc.scalar.activation(out=r, in_=r, func=AF.Exp, scale=-0.5),
                     reads=[rb], writes=[rb])
                for c in range(8):
                    if inplace:
                        o, ob = X[:, c, nsl(n)], XB[c][n]
                    else:
                        o, ob = H[:, c, nsl(n)], HB[c][n]
                    T.op(DVE, lambda: nc.vector.scalar_tensor_tensor(
                        out=o, in0=X[:, c, nsl(n)], scalar=VECS[:, gcol + c:gcol + c + 1], in1=r,
                        op0=ALU.mult, op1=ALU.mult),
                        reads=[XB[c][n], rb, CONSTB], writes=[ob])

        def act_h(k, n):
            return H[:, k, nsl(n)], HB[k][n]

        def p1_fourier_in(l):
            wi = wview(w_in[l])
            handoff(allUB, [b for r in AINB for b in r] + AEB + AOB)
            for jp in range(2):
                wv, wbuf = ring_load(wi[:, :, jp * 256:(jp + 1) * 256], 8, 256)
                for jj in range(2):
                    j = jp * 2 + jj

                    def ev(n, pt, pbf, j=j):
                        if n % 2 == 0:
                            T.op(ACT, lambda: nc.scalar.copy(out=U[:, j, nsl(n)], in_=pt[:]),
                                 reads=[pbf], writes=[AINB[j][n]])
                        else:
                            T.op(DVE, lambda: nc.vector.tensor_copy(out=U[:, j, nsl(n)], in_=pt[:]),
                                 reads=[pbf], writes=[AINB[j][n]])
                    gemm([(wv[:, k, jj * 128:(jj + 1) * 128], wbuf) for k in range(8)], act_h, ev, range(4))
                    a = U[:, j, :]
                    T.op(DVE, lambda: nc.vector.tensor_tensor(out=AE(j)[:, 1:1024], in0=a[:, 1:1024],
                                                              in1=a[:, 2047:1024:-1], op=ALU.add),
                         reads=AINB[j], writes=[AEB[j]])
                    T.op(DVE, lambda: nc.vector.tensor_tensor(out=AO(j)[:, 1:1024], in0=a[:, 1:1024],
                                                              in1=a[:, 2047:1024:-1], op=ALU.subtract),
                         reads=AINB[j], writes=[AOB[j]])
                    T.op(DVE, lambda: nc.vector.tensor_copy(out=AE(j)[:, 0:1], in_=a[:, 0:1]),
                         reads=AINB[j], writes=[AEB[j]])
                    T.op(DVE, lambda: nc.vector.tensor_copy(out=AO(j)[:, 0:1], in_=a[:, 0:1]),
                         reads=AINB[j], writes=[AOB[j]])
            pt, pbf = bank()
            for j in range(4):
                mm(pt[0:1, j * 128:(j + 1) * 128], U[:, j, 1024:1025], BD[:, 0:128], True, True,
                   [AINB[j][2], CONSTB], pbf, j == 3)
            T.op(ACT, lambda: nc.scalar.copy(out=PHT[0:1, :], in_=pt[0:1, :]), reads=[pbf], writes=[PHB])

        def p3_pq():
            handoff([b for r in AINB for b in r], PQB)
            for t in range(8):
                pp, ppb = bank()
                pq, pqb = bank()
                for j in range(4):
                    mm(pp[:, j * 128:(j + 1) * 128], AE(j)[:, t * 128:(t + 1) * 128], BD[:, 0:128],
                       True, True, [AEB[j], CONSTB], ppb, j == 3)
                for j in range(4):
                    mm(pq[:, j * 128:(j + 1) * 128], AO(j)[:, t * 128:(t + 1) * 128], BD[:, 128:256],
                       True, True, [AOB[j], CONSTB], pqb, j == 3)
                T.op(ACT, lambda: nc.scalar.copy(out=PQ(t, 0), in_=pp[:]), reads=[ppb], writes=[PQB[t]])
                T.op(DVE, lambda: nc.vector.tensor_copy(out=PQ(t, 1), in_=pq[:]), reads=[pqb], writes=[PQB[t]])

        def p4_dft():
            handoff(AEB + AOB, FAB)
            for j in range(4):
                fa = U[:, 4 + j, :]
                for q in range(2):
                    pe_, peb = bank()
                    po, pob = bank()
                    ks = slice(q * 512, (q + 1) * 512)
                    for t in range(8):
                        mm(pe_[:], PQ(t, 0, j), CF(t)[:, ks], t == 0, False, [PQB[t], TABB], peb, False)
                    mm(pe_[:], PHT[0:1, j * 128:(j + 1) * 128], ALT[0:1, ks], False, True,
                       [PHB, CONSTB], peb, True)
                    for t in range(8):
                        mm(po[:], PQ(t, 1, j), SF(t)[:, ks], t == 0, t == 7, [PQB[t], TABB], pob, t == 7)
                    osb, ob = s32()
                    T.op(ACT, lambda: nc.scalar.activation(out=osb, in_=po[:], func=AF.Copy, scale=NORM),
                         reads=[pob], writes=[ob])
                    T.op(DVE, lambda: nc.vector.scalar_tensor_tensor(
                        out=fa[:, ks], in0=pe_[:], scalar=NORM, in1=osb, op0=ALU.mult, op1=ALU.subtract),
                        reads=[peb, ob], writes=[FAB[j]])
                    if q == 0:
                        o2, i0, i1 = fa[:, 2047:1536:-1], pe_[:, 1:512], osb[:, 1:512]
                    else:
                        o2, i0, i1 = fa[:, 1536:1024:-1], pe_[:, 0:512], osb[:, 0:512]
                    T.op(DVE, lambda: nc.vector.scalar_tensor_tensor(
                        out=o2, in0=i0, scalar=NORM, in1=i1, op0=ALU.mult, op1=ALU.add),
                        reads=[peb, ob], writes=[FAB[j]])
            pt, pbf = bank()
            for j in range(4):
                for t in range(8):
                    mm(pt[:, j:j + 1], PQ(t, 0, j), CF(t)[:, 1024:1025], t == 0, False, [PQB[t], TABB], pbf, False)
                mm(pt[:, j:j + 1], PHT[0:1, j * 128:(j + 1) * 128], ALT[0:1, 1024:1025], False, True,
                   [PHB, CONSTB], pbf, True)
            T.op(DVE, lambda: nc.vector.tensor_scalar(out=U[:, 4:8, 1024], in0=pt[:, 0:4], scalar1=NORM,
                                                      scalar2=None, op0=ALU.mult),
                 reads=[pbf], writes=FAB)

        def p5_ma(l):
            wi = wview(w_in[l])
            wa = wview(w_a[l])
            handoff([TABB], allMB)
            for m in range(8):
                if m % 4 == 0:
                    wa_v, wa_b = ring_load(wa[:, :, (m // 4) * 512:(m // 4 + 1) * 512], 4, 512)
                if m % 2 == 0:
                    c0 = 2560 + (m // 2) * 256
                    ga_v, ga_b = ring_load(wi[:, :, c0:c0 + 256], 8, 256)
                for npair in ((0, 1), (2, 3)):
                    sig = {}

                    def ev_g(n, pt, pbf):
                        s_, sb_ = s32()
                        sig[n] = (s_, sb_)
                        T.op(ACT, lambda: nc.scalar.activation(out=s_, in_=pt[:], func=AF.Sigmoid),
                             reads=[pbf], writes=[sb_])
                    gemm([(ga_v[:, k, (m % 2) * 128:(m % 2 + 1) * 128], ga_b) for k in range(8)],
                         act_h, ev_g, npair)

                    def ev_y(n, pt, pbf, m=m):
                        s_, sb_ = sig[n]
                        T.op(DVE, lambda: nc.vector.tensor_tensor(out=MRG(m, n), in0=pt[:], in1=s_, op=ALU.mult),
                             reads=[pbf, sb_], writes=[MB[m][n]])
                    gemm([(wa_v[:, j, (m % 4) * 128:(m % 4 + 1) * 128], wa_b) for j in range(4)],
                         lambda j, n: (U[:, 4 + j, nsl(n)], FAB[j]), ev_y, npair)

        def p6_u(l):
            wi = wview(w_in[l])
            handoff(PQB + FAB, allUB)
            for m in range(8):
                if m % 2 == 0:
                    c0 = 512 + (m // 2) * 256
                    wv, wbuf = ring_load(wi[:, :, c0:c0 + 256], 8, 256)

                def ev(n, pt, pbf, m=m):
                    T.op(ACT, lambda: nc.scalar.activation(out=U[:, m, nsl(n)], in_=pt[:], func=AF.Gelu_apprx_tanh),
                         reads=[pbf], writes=[UB[m][n]])
                gemm([(wv[:, k, (m % 2) * 128:(m % 2 + 1) * 128], wbuf) for k in range(8)], act_h, ev, range(4))

        def p7_sgu(l):
            wi = wview(w_in[l])
            lnb_col = l * 24 + 16
            T.dma(POOL, "wst", [lambda: nc.gpsimd.dma_start(out=WST[:], in_=wst_d[l])], writes=[WSTB])
            T.dma(SP, "lng", [lambda: nc.sync.dma_start(out=LNG[:], in_=lng_d[l])], writes=[LNGB])
            T.dma(SP, "bsr", [lambda: nc.sync.dma_start(out=VRAW[:, 0, :], in_=bsr_d[l])], writes=[VRB[0]])
            for half in range(2):
                pt, pbf = bank()
                for hq in range(4):
                    hh = half * 4 + hq
                    mm(pt[:, hq * 128:(hq + 1) * 128], ONES[:], WST[:, hh * 128:(hh + 1) * 128], True, True,
                       [ONESB, WSTB], pbf, hq == 3)
                for hq in range(4):
                    hh = half * 4 + hq
                    T.op(DVE, lambda: nc.vector.scalar_tensor_tensor(
                        out=TT[:, hh * 128:(hh + 1) * 128], in0=pt[:, hq * 128:(hq + 1) * 128],
                        scalar=VECS[:, lnb_col + hh:lnb_col + hh + 1], in1=VRAW[:, 0, hh * 128:(hh + 1) * 128],
                        op0=ALU.mult, op1=ALU.add),
                        reads=[pbf, VRB[0], CONSTB], writes=[TTB])
            wv = []
            for c2 in range(4):
                c0 = 1536 + c2 * 256
                wv.append(ring_load(wi[:, :, c0:c0 + 256], 8, 256))

            vbanks = {}
            sbanks = {}

            def stageA(t):
                vb = [bank(), bank()]
                vbanks[t] = vb
                for c2 in range(4):
                    pt, pbf = vb[c2 // 2]
                    for k in range(8):
                        mm(pt[:, (c2 % 2) * 256:(c2 % 2 + 1) * 256], H[:, k, t * 128:(t + 1) * 128],
                           wv[c2][0][:, k, :], k == 0, k == 7, [HB[k][t // 4], wv[c2][1]], pbf,
                           k == 7 and c2 % 2 == 1)

            def stageB(t):
                sl = t % 2
                vs = t % 3
                vb = vbanks.pop(t)
                for hf in range(2):
                    pt, pbf = vb[hf]
                    T.op(ACT, lambda: nc.scalar.activation(out=VRAW[:, sl, hf * 512:(hf + 1) * 512], in_=pt[:],
                                                           func=AF.Gelu_apprx_tanh),
                         reads=[pbf], writes=[VRB[sl]])
                for hf in range(2):
                    T.op(DVE, lambda: nc.vector.bn_stats(out=STATS[:, sl, hf * 6:(hf + 1) * 6],
                                                         in_=VRAW[:, sl, hf * 512:(hf + 1) * 512]),
                         reads=[VRB[sl]], writes=[STB[sl]])
                T.op(DVE, lambda: nc.vector.bn_aggr(out=MV[:, sl, :], in_=STATS[:, sl, :]),
                     reads=[STB[sl]], writes=[STB[sl]])
                T.op(ACT, lambda: nc.scalar.activation(out=RS[:, sl, 0:1], in_=MV[:, sl, 1:2], func=AF.Ln,
                                                       scale=1.0, bias=EPS),
                     reads=[STB[sl]], writes=[STB[sl]])
                T.op(ACT, lambda: nc.scalar.activation(out=RS[:, sl, 0:1], in_=RS[:, sl, 0:1], func=AF.Exp,
                                                       scale=-0.5),
                     reads=[STB[sl]], writes=[STB[sl]])
                T.op(DVE, lambda: nc.vector.scalar_tensor_tensor(
                    out=RS[:, sl, 1:2], in0=MV[:, sl, 0:1], scalar=-1.0, in1=RS[:, sl, 0:1],
                    op0=ALU.mult, op1=ALU.mult),
                    reads=[STB[sl]], writes=[STB[sl]])
                T.op(ACT, lambda: nc.scalar.activation(out=VRAW[:, sl, :], in_=VRAW[:, sl, :], func=AF.Identity,
                                                       scale=RS[:, sl, 0:1], bias=RS[:, sl, 1:2]),
                     reads=[VRB[sl], STB[sl]], writes=[VRB[sl]])
                T.op(DVE, lambda: nc.vector.tensor_tensor(out=VN[:, vs, :], in0=VRAW[:, sl, :], in1=LNG[:],
                                                          op=ALU.mult),
                     reads=[VRB[sl], LNGB], writes=[VNB[vs]])

            def stageC(t):
                vs = t % 3
                sp_ = [bank(), bank()]
                sbanks[t] = sp_
                for hh in range(8):
                    pt, pbf = sp_[hh // 4]
                    mm(pt[:, (hh % 4) * 128:(hh % 4 + 1) * 128], VN[:, vs, hh * 128:(hh + 1) * 128],
                       WST[:, hh * 128:(hh + 1) * 128], True, True, [VNB[vs], WSTB], pbf, hh % 4 == 3)

            def stageD(t):
                sp_ = sbanks.pop(t)
                for half in range(2):
                    pt, pbf = sp_[half]
                    tmp, tb = s32()
                    T.op(DVE, lambda: nc.vector.tensor_tensor(out=tmp, in0=pt[:],
                                                              in1=TT[:, half * 512:(half + 1) * 512], op=ALU.add),
                         reads=[pbf, TTB], writes=[tb])
                    uu = U[:, half * 4:(half + 1) * 4, t * 128:(t + 1) * 128]
                    ubs = [UB[half * 4 + i][t // 4] for i in range(4)]
                    T.op(DVE, lambda: nc.vector.tensor_tensor(
                        out=uu, in0=tmp.rearrange("p (a b) -> p a b", a=4), in1=uu, op=ALU.mult),
                        reads=[tb] + ubs, writes=ubs)

            LAG = 2
            for i in range(16 + LAG):
                if i < 16:
                    stageA(i)
                    stageB(i)
                if i >= LAG:
                    stageC(i - LAG)
                    stageD(i - LAG)

        def p8_merge(l):
            wi = wview(w_in[l])
            wb_ = wview(w_b[l])
            for m in range(8):
                if m % 2 == 0:
                    wb_v, wb_b = ring_load(wb_[:, :, (m // 2) * 256:(m // 2 + 1) * 256], 8, 256)
                    c0 = 3584 + (m // 2) * 256
                    gb_v, gb_b = ring_load(wi[:, :, c0:c0 + 256], 8, 256)
                for npair in ((0, 1), (2, 3)):
                    sig = {}

                    def ev_g(n, pt, pbf):
                        s_, sb_ = s32()
                        sig[n] = (s_, sb_)
                        T.op(ACT, lambda: nc.scalar.activation(out=s_, in_=pt[:], func=AF.Sigmoid),
                             reads=[pbf], writes=[sb_])
                    gemm([(gb_v[:, k, (m % 2) * 128:(m % 2 + 1) * 128], gb_b) for k in range(8)],
                         act_h, ev_g, npair)

                    def ev_y(n, pt, pbf, m=m):
                        s_, sb_ = sig[n]
                        T.op(DVE, lambda: nc.vector.tensor_tensor(out=s_, in0=pt[:], in1=s_, op=ALU.mult),
                             reads=[pbf, sb_], writes=[sb_])
                        T.op(DVE, lambda: nc.vector.tensor_tensor(out=MRG(m, n), in0=MRG(m, n), in1=s_, op=ALU.add),
                             reads=[sb_, MB[m][n]], writes=[MB[m][n]])
                    gemm([(wb_v[:, k, (m % 2) * 128:(m % 2 + 1) * 128], wb_b) for k in range(8)],
                         lambda k, n: (U[:, k, nsl(n)], UB[k][n]), ev_y, npair)

        def xadd_evac(m):
            def ev(n, pt, pbf):
                T.op(DVE, lambda: nc.vector.tensor_tensor(out=X[:, m, nsl(n)], in0=pt[:], in1=X[:, m, nsl(n)],
                                                          op=ALU.add),
                     reads=[pbf, XB[m][n]], writes=[XB[m][n]])
            return ev

        def p9_out(l):
            wo = wview(w_out[l])
            for m in range(8):
                if m % 2 == 0:
                    wv, wbuf = ring_load(wo[:, :, (m // 2) * 256:(m // 2 + 1) * 256], 8, 256)
                gemm([(wv[:, k, (m % 2) * 128:(m % 2 + 1) * 128], wbuf) for k in range(8)],
                     lambda k, n: (MRG(k, n), MB[k][n]), xadd_evac(m), range(4))

        def p11_mlp(l):
            wu = wview(w_up[l])
            for q in range(4):
                for fc in range(8):
                    if fc % 2 == 0:
                        c0 = q * 1024 + (fc // 2) * 256
                        wv, wbuf = ring_load(wu[:, :, c0:c0 + 256], 8, 256)

                    def ev(n, pt, pbf, fc=fc):
                        r, rb = s32()
                        o = U[:, fc, nsl(n)]
                        if (fc + n) % 2 == 0:
                            T.op(ACT, lambda: nc.scalar.activation(out=r, in_=pt[:], func=AF.Relu),
                                 reads=[pbf], writes=[rb])
                            T.op(ACT, lambda: nc.scalar.activation(out=o, in_=r, func=AF.Square),
                                 reads=[rb], writes=[UB[fc][n]])
                        else:
                            T.op(DVE, lambda: nc.vector.tensor_scalar(out=r, in0=pt[:], scalar1=0.0, scalar2=None,
                                                                      op0=ALU.max),
                                 reads=[pbf], writes=[rb])
                            T.op(DVE, lambda: nc.vector.tensor_tensor(out=o, in0=r, in1=r, op=ALU.mult),
                                 reads=[rb], writes=[UB[fc][n]])
                    gemm([(wv[:, k, (fc % 2) * 128:(fc % 2 + 1) * 128], wbuf) for k in range(8)], act_h, ev, range(4))
                wd = wview(w_down[l][q * 1024:(q + 1) * 1024, :])
                for m in range(8):
                    if m % 2 == 0:
                        wv2, wbuf2 = ring_load(wd[:, :, (m // 2) * 256:(m // 2 + 1) * 256], 8, 256)
                    gemm([(wv2[:, k, (m % 2) * 128:(m % 2 + 1) * 128], wbuf2) for k in range(8)],
                         lambda k, n: (U[:, k, nsl(n)], UB[k][n]), xadd_evac(m), range(4))

        def load_tables():
            handoff(allMB, [TABB])
            T.dma(SP, "tab",
                  [lambda: nc.sync.dma_start(out=MRGR[:, 0:8 * TK].rearrange("p (t k) -> p t k", t=8),
                                             in_=cf_d.rearrange("(t p) k -> p t k", p=128)),
                   lambda: nc.sync.dma_start(out=MRGR[:, 8 * TK:16 * TK].rearrange("p (t k) -> p t k", t=8),
                                             in_=sf_d.rearrange("(t p) k -> p t k", p=128))],
                  writes=[TABB])

        for b in range(nseq):
            xv = xT[b].rearrange("(c p) t -> p c t", p=128)
            T.dma(SP, "xld", [(lambda c=c: nc.sync.dma_start(out=X[:, c, :], in_=xv[:, c, :])) for c in range(8)],
                  writes=allXB)
            for l in range(L):
                load_tables()
                rmsnorm(l * 24 + 0, False)
                p1_fourier_in(l)
                p3_pq()
                p4_dft()
                p5_ma(l)
                p6_u(l)
                p7_sgu(l)
                p8_merge(l)
                p9_out(l)
                rmsnorm(l * 24 + 8, False)
                p11_mlp(l)
            if final:
                rmsnorm(L * 24, True)
            ov = outT[b].rearrange("(c p) t -> p c t", p=128)
            T.dma(SP, "xst", [(lambda c=c: nc.sync.dma_start(out=ov[:, c, :], in_=X[:, c, :])) for c in range(8)],
                  reads=allXB)
        nc.sync.wait_ge(T.sems["xst"], T.dcnt["xst"])
    return nc


_CACHE = {}


def _get_nc(L, final):
    key = (L, final)
    if key not in _CACHE:
        _CACHE[key] = build(L, final)
    return _CACHE[key]


def _consts():
    s = np.arange(1024, dtype=np.float64)[:, None]
    k = np.arange(TK, dtype=np.float64)[None, :]
    ang = 2.0 * np.pi * ((s * k) % 2048.0) / 2048.0
    cf = np.cos(ang)
    sf = np.sin(ang)
    cf[:, 1025:] = 0.0
    sf[:, 1025:] = 0.0
    c = np.arange(64, dtype=np.float64)[:, None]
    m = np.arange(64, dtype=np.float64)[None, :]
    a64 = 2.0 * np.pi * ((c * m) % 64.0) / 64.0
    bd = np.zeros((128, 256), np.float64)
    for g in range(2):
        bd[g * 64:(g + 1) * 64, g * 64:(g + 1) * 64] = np.cos(a64)
        bd[g * 64:(g + 1) * 64, 128 + g * 64:128 + (g + 1) * 64] = np.sin(a64)
    alt = np.zeros((1, TK), np.float64)
    alt[0, :1025] = np.where(np.arange(1025) % 2 == 0, 1.0, -1.0)
    bf = ml_dtypes.bfloat16
    return {"cf": cf.astype(np.float32).astype(bf), "sf": sf.astype(np.float32).astype(bf),
            "bd": bd.astype(np.float32).astype(bf), "alt": alt.astype(np.float32).astype(bf)}


def _layer_inputs(inp, layers, with_final):
    L = len(layers)
    f32 = np.float32
    sel = lambda a: np.ascontiguousarray(np.asarray(a, f32)[layers])
    w_s = np.asarray(inp["w_s"], f32)[layers]
    wst = np.ascontiguousarray(w_s.transpose(0, 3, 1, 2).reshape(L, 128, 1024))
    lng = np.ascontiguousarray(np.broadcast_to(np.asarray(inp["ln_v_g"], f32)[layers][:, None, :], (L, 128, 1024)))
    bsr = np.ascontiguousarray(np.broadcast_to(
        np.asarray(inp["b_s"], f32)[layers].reshape(L, 1, 1024), (L, 128, 1024)))
    vecs = np.zeros((128, L * 24 + 8), f32)
    fm = lambda v: np.asarray(v, f32).reshape(8, 128).T
    for i, l in enumerate(layers):
        vecs[:, i * 24 + 0:i * 24 + 8] = fm(inp["g_mix"][l])
        vecs[:, i * 24 + 8:i * 24 + 16] = fm(inp["g_mlp"][l])
        vecs[:, i * 24 + 16:i * 24 + 24] = fm(inp["ln_v_b"][l])
    vecs[:, L * 24:L * 24 + 8] = fm(inp["g_final"])
    d = {"w_in": sel(inp["w_in"]), "w_a": sel(inp["w_a"]), "w_b": sel(inp["w_b"]), "w_out": sel(inp["w_out"]),
         "w_up": sel(inp["w_up"]), "w_down": sel(inp["w_down"]), "wst": wst, "lng": lng, "bsr": bsr, "vecs": vecs}
    d.update(_consts())
    return d


def kernel(**inp):
    x = np.asarray(inp["x"], np.float32)
    xT = np.ascontiguousarray(x.transpose(0, 2, 1))
    if MODE == "fused":
        plan = [(list(range(DEPTH)), True)]
    else:
        plan = [([l], l == DEPTH - 1) for l in range(DEPTH)]
    cur = xT
    for layers, fin in plan:
        nc = _get_nc(len(layers), fin)
        shared = _layer_inputs(inp, layers, fin)
        in_maps = []
        for c in range(N_CORES):
            m = dict(shared)
            m["xT"] = np.ascontiguousarray(cur[c * NSEQ:(c + 1) * NSEQ])
            in_maps.append(m)
        res = run_bass_kernel_spmd(nc, in_maps, core_ids=list(range(N_CORES)))
        cur = np.concatenate([np.asarray(r["outT"], np.float32) for r in res.results], axis=0)
    return np.ascontiguousarray(cur.transpose(0, 2, 1))
```

```python
import math
from contextlib import ExitStack

import numpy as np
import ml_dtypes

import concourse.bass as bass
import concourse.mybir as mybir
from concourse.bass_utils import run_bass_kernel_spmd

F32 = mybir.dt.float32
BF16 = mybir.dt.bfloat16
AF = mybir.ActivationFunctionType
ALU = mybir.AluOpType

N_CORES = 8
DEPTH = 4
D = 1024
S = 2048
NSEQ = 4
NT = 4
TW = 512
RING_SLOTS = 5
RING_ELEMS = 2048
EPS = 1e-6
NORM = 1.0 / math.sqrt(2048.0 * 64.0)
TK = 1028

MODE = "fused"


class Buf:
    __slots__ = ("w", "r")

    def __init__(self):
        self.w = None
        self.r = {}


class Eng:
    def __init__(self, handle, key, is_pe=False):
        self.h = handle
        self.key = key
        self.cnt = 0
        self.seen = {}
        self.is_pe = is_pe


class Tracker:
    def __init__(self, nc, es):
        self.nc = nc
        self.es = es
        self.sems = {}
        self.dcnt = {}
        self.phase = "init"

    def sem(self, key):
        if key not in self.sems:
            self.sems[key] = self.es.enter_context(self.nc.semaphore(key))
            self.dcnt[key] = 0
        return self.sems[key]

    @staticmethod
    def _collect(reads, writes):
        deps = {}
        for b in reads:
            if b.w is not None:
                k, v = b.w
                if deps.get(k, 0) < v:
                    deps[k] = v
        for b in writes:
            if b.w is not None:
                k, v = b.w
                if deps.get(k, 0) < v:
                    deps[k] = v
            for k, v in b.r.items():
                if deps.get(k, 0) < v:
                    deps[k] = v
        return deps

    def _wait(self, E, deps):
        for k, v in deps.items():
            if E.is_pe and k == E.key:
                continue
            if E.seen.get(k, 0) >= v:
                continue
            E.h.wait_ge(self.sems[k], v)
            E.seen[k] = v

    @staticmethod
    def _mark(tag, reads, writes):
        k, v = tag
        for b in reads:
            if b.r.get(k, 0) < v:
                b.r[k] = v
        for b in writes:
            b.w = tag
            b.r = {}

    def op(self, E, fn, reads=(), writes=(), signal=True):
        self._wait(E, self._collect(reads, writes))
        inst = fn().annotate(self.phase)
        if signal:
            inst.then_inc(self.sem(E.key), 1)
            E.cnt += 1
            tag = (E.key, E.cnt)
        else:
            tag = (E.key, E.cnt + 1)
        self._mark(tag, reads, writes)

    def dma(self, Q, semkey, fns, reads=(), writes=()):
        sem = self.sem(semkey)
        self._wait(Q, self._collect(reads, writes))
        for fn in fns:
            fn().then_inc(sem, 16)
            self.dcnt[semkey] += 16
        self._mark((semkey, self.dcnt[semkey]), reads, writes)


def handoff(src, dst):
    merged = {}
    for b in src:
        if b.w is not None:
            k, v = b.w
            if merged.get(k, 0) < v:
                merged[k] = v
        for k, v in b.r.items():
            if merged.get(k, 0) < v:
                merged[k] = v
    for d in dst:
        d.w = None
        d.r = dict(merged)


def build(L, final, nseq=NSEQ):
    nc = bass.Bass("TRN2", target_bir_lowering=False, dynamic_dma_scratch_size=4096)

    def dram(name, shape, dt=F32, kind="ExternalInput"):
        return nc.dram_tensor(name, shape, dt, kind=kind).ap()

    xT = dram("xT", [nseq, D, S])
    outT = dram("outT", [nseq, D, S], kind="ExternalOutput")
    w_in = dram("w_in", [L, D, 4608])
    w_a = dram("w_a", [L, 512, D])
    w_b = dram("w_b", [L, D, D])
    w_out = dram("w_out", [L, D, D])
    w_up = dram("w_up", [L, D, 4096])
    w_down = dram("w_down", [L, 4096, D])
    wst_d = dram("wst", [L, 128, 1024])
    bsr_d = dram("bsr", [L, 128, 1024])
    lng_d = dram("lng", [L, 128, 1024])
    NV = L * 32 + 8
    vecs_d = dram("vecs", [128, NV])
    cf_d = dram("cf", [1024, TK], BF16)
    sf_d = dram("sf", [1024, TK], BF16)
    bd_d = dram("bd", [128, 256], BF16)
    alt_d = dram("alt", [1, TK], BF16)

    with ExitStack() as es:
        def sb(name, shape, dt):
            return es.enter_context(nc.sbuf_tensor(name, shape, dt))

        X = sb("X", [128, 8, S], F32)
        H = sb("H", [128, 8, S], BF16)
        U = sb("U", [128, 8, S], BF16)
        MRGR = sb("MRGR", [128, 2 * 8 * TK], BF16)
        RING = sb("RING", [128, RING_SLOTS, RING_ELEMS], BF16)
        S32 = sb("S32", [128, 4, TW], F32)
        S16 = sb("S16", [128, 2, TW], BF16)
        VRAW = sb("VRAW", [128, 2, 1024], F32)
        VN = sb("VN", [128, 3, 1024], BF16)
        TT = sb("TT", [128, 1024], F32)
        WST = sb("WST", [128, 1024], BF16)
        LNG = sb("LNG", [128, 1024], F32)
        VECS = sb("VECS", [128, NV], F32)
        BD = sb("BD", [128, 256], BF16)
        ONES = sb("ONES", [128, 128], BF16)
        ALT = sb("ALT", [1, TK], BF16)
        PHT = sb("PHT", [1, 512], BF16)
        STATS = sb("STATS", [128, 2, 8], F32)
        RS = sb("RS", [128, 2, 2], F32)
        MHALF = sb("MHALF", [128, 1], F32)
        SQS = sb("SQS", [128, 2, 2], F32)
        ps = [es.enter_context(nc.psum_tensor(f"ps{i}", [128, TW], F32)) for i in range(8)]

        T = Tracker(nc, es)

        PE = Eng(nc.tensor, "pe", is_pe=True)
        ACT = Eng(nc.scalar, "act")
        DVE = Eng(nc.vector, "dve")
        SP = Eng(nc.sync, "sp")
        POOL = Eng(nc.gpsimd, "pool")
        for e in (PE, ACT, DVE):
            T.sem(e.key)

        XB = [[Buf() for _ in range(NT)] for _ in range(8)]
        HB = [[Buf() for _ in range(NT)] for _ in range(8)]
        UB = [[Buf() for _ in range(NT)] for _ in range(8)]
        MB = [[Buf() for _ in range(NT)] for _ in range(8)]
        AINB = [[Buf() for _ in range(NT)] for _ in range(4)]
        AEB = [Buf() for _ in range(4)]
        AOB = [Buf() for _ in range(4)]
        PQB = [Buf() for _ in range(8)]
        FAB = [Buf() for _ in range(4)]
        TABB = Buf()
        PHB = Buf()
        CONSTB = Buf()
        ONESB = Buf()
        VRB = [Buf(), Buf()]
        VNB = [Buf(), Buf(), Buf()]
        STB = [Buf(), Buf()]
        SQB = [Buf(), Buf()]
        TTB = Buf()
        WSTB = Buf()
        LNGB = Buf()
        PB = [Buf() for _ in range(8)]
        S32B = [Buf() for _ in range(4)]
        S16B = [Buf() for _ in range(2)]
        RINGB = [Buf() for _ in range(RING_SLOTS)]
        allUB = [b for r in UB for b in r]
        allMB = [b for r in MB for b in r]
        allXB = [b for r in XB for b in r]

        st = {"bank": 0, "s32": 0, "s16": 0, "ring": 0}

        def bank():
            i = st["bank"]
            st["bank"] = (i + 1) % 8
            return ps[i], PB[i]

        def s32():
            i = st["s32"]
            st["s32"] = (i + 1) % 4
            return S32[:, i, :], S32B[i]

        def s16():
            i = st["s16"]
            st["s16"] = (i + 1) % 2
            return S16[:, i, :], S16B[i]

        def ring_load(src, kc, cols):
            i = st["ring"]
            st["ring"] = (i + 1) % RING_SLOTS
            view = RING[:, i, 0:kc * cols].rearrange("p (k c) -> p k c", k=kc)
            T.dma(POOL, f"ring{i}", [lambda: nc.gpsimd.dma_start(out=view, in_=src)],
                  writes=[RINGB[i]])
            return view, RINGB[i]

        PRE = {}

        def get_piece(key, src, kc, cols):
            if key in PRE:
                return PRE.pop(key)
            return ring_load(src, kc, cols)

        def nsl(n):
            return slice(n * TW, (n + 1) * TW)

        def mm(out, lhsT, rhs, start, stop, reads, pbuf, signal):
            T.op(PE, lambda: nc.tensor.matmul(out, lhsT, rhs, start=start, stop=stop),
                 reads=reads, writes=[pbuf], signal=signal)

        def gemm(wl, act, evac, ns):
            bks = {n: bank() for n in ns}
            K = len(wl)
            for k in range(K):
                lh, lb = wl[k]
                for n in ns:
                    ra, rb = act(k, n)
                    pt, pbf = bks[n]
                    mm(pt[:], lh, ra, k == 0, k == K - 1, [lb, rb], pbf, k == K - 1)
            for n in ns:
                evac(n, *bks[n])

        def wview(w2d):
            return w2d.rearrange("(k p) m -> p k m", p=128)

        def AE(j):
            return U[:, 4 + j // 2, (j % 2) * 1024:(j % 2 + 1) * 1024]

        def AO(j):
            return U[:, 6 + j // 2, (j % 2) * 1024:(j % 2 + 1) * 1024]

        def PQ(t, which, j=None):
            base = (t % 2) * 1024 + which * 512
            if j is None:
                return U[:, t // 2, base:base + 512]
            return U[:, t // 2, base + j * 128:base + (j + 1) * 128]

        def CF(t):
            return MRGR[:, t * TK:(t + 1) * TK]

        def SF(t):
            return MRGR[:, 8 * TK + t * TK:8 * TK + (t + 1) * TK]

        def MRG(m, n):
            return MRGR[:, m * S + n * TW:m * S + (n + 1) * TW]

        T.dma(SP, "c0", [lambda: nc.sync.dma_start(out=VECS[:], in_=vecs_d),
                         lambda: nc.sync.dma_start(out=BD[:], in_=bd_d),
                         lambda: nc.sync.dma_start(out=ALT[:], in_=alt_d)],
              writes=[CONSTB])
        T.op(DVE, lambda: nc.vector.memset(ONES[:], 1.0), writes=[ONESB])
        T.op(DVE, lambda: nc.vector.memset(MHALF[:], -0.5), writes=[ONESB])

        def rmsnorm(gcol, inplace):
            for n in range(NT):
                rms_n(gcol, inplace, n)

        def rms_n(gcol, inplace, n):
            ph_save = T.phase
            T.phase = "rms"
            if True:
                pst, pbf = bank()
                for c in range(8):
                    sq, sqb = s16()
                    T.op(ACT, lambda: nc.scalar.activation(out=sq, in_=X[:, c, nsl(n)], func=AF.Square),
                         reads=[XB[c][n]], writes=[sqb])
                    mm(pst[:], ONES[:], sq, c == 0, c == 7, [sqb, ONESB], pbf, True)
                r, rb = s32()
                T.op(ACT, lambda: nc.scalar.activation(out=r, in_=pst[:], func=AF.Ln,
                                                       scale=1.0 / D, bias=EPS),
                     reads=[pbf], writes=[rb])
                T.op(ACT, lambda: nc.scalar.activation(out=r, in_=r, func=AF.Exp, scale=-0.5),
                     reads=[rb], writes=[rb])
                for c in range(8):
                    if inplace:
                        o, ob = X[:, c, nsl(n)], XB[c][n]
                    else:
                        o, ob = H[:, c, nsl(n)], HB[c][n]
                    T.op(DVE, lambda: nc.vector.scalar_tensor_tensor(
                        out=o, in0=X[:, c, nsl(n)], scalar=VECS[:, gcol + c:gcol + c + 1], in1=r,
                        op0=ALU.mult, op1=ALU.mult),
                        reads=[XB[c][n], rb, CONSTB], writes=[ob])
            T.phase = ph_save

        def act_h(k, n):
            return H[:, k, nsl(n)], HB[k][n]

        def p1_fourier_in(l, gcol):
            wi = wview(w_in[l])
            handoff(allUB, [b for r in AINB for b in r] + AEB + AOB)
            pcs = [ring_load(wi[:, :, jp * 256:(jp + 1) * 256], 8, 256) for jp in range(2)]
            rms_n(gcol, False, 0)
            rms_n(gcol, False, 1)
            for n in range(NT):
                if n >= 1 and n + 1 < NT:
                    rms_n(gcol, False, n + 1)
                for j in range(4):
                    wv, wbuf = pcs[j // 2]
                    jj = j % 2

                    def ev(n_, pt, pbf, j=j):
                        if j % 2 == 0:
                            T.op(ACT, lambda: nc.scalar.copy(out=U[:, j, nsl(n_)], in_=pt[:]),
                                 reads=[pbf], writes=[AINB[j][n_]])
                        else:
                            T.op(DVE, lambda: nc.vector.tensor_copy(out=U[:, j, nsl(n_)], in_=pt[:]),
                                 reads=[pbf], writes=[AINB[j][n_]])
                    gemm([(wv[:, k, jj * 128:(jj + 1) * 128], wbuf) for k in range(8)], act_h, ev, [n])
            for j in range(4):
                a = U[:, j, :]
                T.op(DVE, lambda: nc.vector.tensor_tensor(out=AE(j)[:, 1:1024], in0=a[:, 1:1024],
                                                          in1=a[:, 2047:1024:-1], op=ALU.add),
                     reads=AINB[j], writes=[AEB[j]])
                T.op(DVE, lambda: nc.vector.tensor_tensor(out=AO(j)[:, 1:1024], in0=a[:, 1:1024],
                                                          in1=a[:, 2047:1024:-1], op=ALU.subtract),
                     reads=AINB[j], writes=[AOB[j]])
                T.op(DVE, lambda: nc.vector.tensor_copy(out=AE(j)[:, 0:1], in_=a[:, 0:1]),
                     reads=AINB[j], writes=[AEB[j]])
                T.op(DVE, lambda: nc.vector.tensor_copy(out=AO(j)[:, 0:1], in_=a[:, 0:1]),
                     reads=AINB[j], writes=[AOB[j]])
            pt, pbf = bank()
            for j in range(4):
                mm(pt[0:1, j * 128:(j + 1) * 128], U[:, j, 1024:1025], BD[:, 0:128], True, True,
                   [AINB[j][2], CONSTB], pbf, j == 3)
            T.op(ACT, lambda: nc.scalar.copy(out=PHT[0:1, :], in_=pt[0:1, :]), reads=[pbf], writes=[PHB])

        def p3_pq():
            handoff([b for r in AINB for b in r], PQB)
            for t in range(8):
                pp, ppb = bank()
                pq, pqb = bank()
                for j in range(4):
                    mm(pp[:, j * 128:(j + 1) * 128], AE(j)[:, t * 128:(t + 1) * 128], BD[:, 0:128],
                       True, True, [AEB[j], CONSTB], ppb, j == 3)
                for j in range(4):
                    mm(pq[:, j * 128:(j + 1) * 128], AO(j)[:, t * 128:(t + 1) * 128], BD[:, 128:256],
                       True, True, [AOB[j], CONSTB], pqb, j == 3)
                T.op(ACT, lambda: nc.scalar.copy(out=PQ(t, 0), in_=pp[:]), reads=[ppb], writes=[PQB[t]])
                T.op(DVE, lambda: nc.vector.tensor_copy(out=PQ(t, 1), in_=pq[:]), reads=[pqb], writes=[PQB[t]])

        def p4_dft():
            handoff(AEB + AOB, FAB)
            for j in range(4):
                fa = U[:, 4 + j, :]
                for q in range(2):
                    pe_, peb = bank()
                    po, pob = bank()
                    ks = slice(q * 512, (q + 1) * 512)
                    for t in range(8):
                        mm(pe_[:], PQ(t, 0, j), CF(t)[:, ks], t == 0, False, [PQB[t], TABB], peb, False)
                    mm(pe_[:], PHT[0:1, j * 128:(j + 1) * 128], ALT[0:1, ks], False, True,
                       [PHB, CONSTB], peb, True)
                    for t in range(8):
                        mm(po[:], PQ(t, 1, j), SF(t)[:, ks], t == 0, t == 7, [PQB[t], TABB], pob, t == 7)
                    osb, ob = s32()
                    T.op(ACT, lambda: nc.scalar.activation(out=osb, in_=po[:], func=AF.Copy, scale=NORM),
                         reads=[pob], writes=[ob])
                    T.op(DVE, lambda: nc.vector.scalar_tensor_tensor(
                        out=fa[:, ks], in0=pe_[:], scalar=NORM, in1=osb, op0=ALU.mult, op1=ALU.subtract),
                        reads=[peb, ob], writes=[FAB[j]])
                    if q == 0:
                        o2, i0, i1 = fa[:, 2047:1536:-1], pe_[:, 1:512], osb[:, 1:512]
                    else:
                        o2, i0, i1 = fa[:, 1536:1024:-1], pe_[:, 0:512], osb[:, 0:512]
                    T.op(DVE, lambda: nc.vector.scalar_tensor_tensor(
                        out=o2, in0=i0, scalar=NORM, in1=i1, op0=ALU.mult, op1=ALU.add),
                        reads=[peb, ob], writes=[FAB[j]])
            pt, pbf = bank()
            for j in range(4):
                for t in range(8):
                    mm(pt[:, j:j + 1], PQ(t, 0, j), CF(t)[:, 1024:1025], t == 0, False, [PQB[t], TABB], pbf, False)
                mm(pt[:, j:j + 1], PHT[0:1, j * 128:(j + 1) * 128], ALT[0:1, 1024:1025], False, True,
                   [PHB, CONSTB], pbf, True)
            T.op(DVE, lambda: nc.vector.tensor_scalar(out=U[:, 4:8, 1024], in0=pt[:, 0:4], scalar1=NORM,
                                                      scalar2=None, op0=ALU.mult),
                 reads=[pbf], writes=FAB)

        def p5_ma(l):
            wi = wview(w_in[l])
            wa = wview(w_a[l])
            handoff([TABB], allMB)
            for m in range(8):
                if m % 4 == 0:
                    wa_v, wa_b = ring_load(wa[:, :, (m // 4) * 512:(m // 4 + 1) * 512], 4, 512)
                if m % 2 == 0:
                    c0 = 2560 + (m // 2) * 256
                    ga_v, ga_b = ring_load(wi[:, :, c0:c0 + 256], 8, 256)
                for npair in ((0, 1), (2, 3)):
                    sig = {}

                    def ev_g(n, pt, pbf):
                        s_, sb_ = s32()
                        sig[n] = (s_, sb_)
                        T.op(ACT, lambda: nc.scalar.activation(out=s_, in_=pt[:], func=AF.Sigmoid),
                             reads=[pbf], writes=[sb_])
                    gemm([(ga_v[:, k, (m % 2) * 128:(m % 2 + 1) * 128], ga_b) for k in range(8)],
                         act_h, ev_g, npair)

                    def ev_y(n, pt, pbf, m=m):
                        s_, sb_ = sig[n]
                        T.op(DVE, lambda: nc.vector.tensor_tensor(out=MRG(m, n), in0=pt[:], in1=s_, op=ALU.mult),
                             reads=[pbf, sb_], writes=[MB[m][n]])
                    gemm([(wa_v[:, j, (m % 4) * 128:(m % 4 + 1) * 128], wa_b) for j in range(4)],
                         lambda j, n: (U[:, 4 + j, nsl(n)], FAB[j]), ev_y, npair)

        def p6_u(l):
            wi = wview(w_in[l])
            handoff(PQB + FAB, allUB)
            for m in range(8):
                if m % 2 == 0:
                    c0 = 512 + (m // 2) * 256
                    wv, wbuf = ring_load(wi[:, :, c0:c0 + 256], 8, 256)

                def ev(n, pt, pbf, m=m):
                    T.op(ACT, lambda: nc.scalar.activation(out=U[:, m, nsl(n)], in_=pt[:], func=AF.Gelu_apprx_tanh),
                         reads=[pbf], writes=[UB[m][n]])
                gemm([(wv[:, k, (m % 2) * 128:(m % 2 + 1) * 128], wbuf) for k in range(8)], act_h, ev, range(4))

        def p7_sgu(l):
            wi = wview(w_in[l])
            lnb_col = l * 32 + 16
            lng_col = l * 32 + 24
            T.dma(POOL, "wst", [lambda: nc.gpsimd.dma_start(out=WST[:], in_=wst_d[l])], writes=[WSTB])
            T.dma(SP, "bsr", [lambda: nc.sync.dma_start(out=VRAW[:, 0, :], in_=bsr_d[l])], writes=[VRB[0]])
            T.dma(SP, "lng", [lambda: nc.sync.dma_start(out=LNG[:], in_=lng_d[l])], writes=[LNGB])
            for half in range(2):
                pt, pbf = bank()
                for hq in range(4):
                    hh = half * 4 + hq
                    mm(pt[:, hq * 128:(hq + 1) * 128], ONES[:], WST[:, hh * 128:(hh + 1) * 128], True, True,
                       [ONESB, WSTB], pbf, hq == 3)
                for hq in range(4):
                    hh = half * 4 + hq
                    T.op(DVE, lambda: nc.vector.scalar_tensor_tensor(
                        out=TT[:, hh * 128:(hh + 1) * 128], in0=pt[:, hq * 128:(hq + 1) * 128],
                        scalar=VECS[:, lnb_col + hh:lnb_col + hh + 1], in1=VRAW[:, 0, hh * 128:(hh + 1) * 128],
                        op0=ALU.mult, op1=ALU.add),
                        reads=[pbf, VRB[0], CONSTB], writes=[TTB])
            wv = []
            for c2 in range(4):
                c0 = 1536 + c2 * 256
                wv.append(ring_load(wi[:, :, c0:c0 + 256], 8, 256))

            vbanks = {}
            sbanks = {}

            def stageA(t):
                vb = [bank(), bank()]
                vbanks[t] = vb
                for c2 in range(4):
                    pt, pbf = vb[c2 // 2]
                    for k in range(8):
                        mm(pt[:, (c2 % 2) * 256:(c2 % 2 + 1) * 256], H[:, k, t * 128:(t + 1) * 128],
                           wv[c2][0][:, k, :], k == 0, k == 7, [HB[k][t // 4], wv[c2][1]], pbf,
                           k == 7 and c2 % 2 == 1)

            def stageB1(t):
                sl = t % 2
                vs = t % 3
                vb = vbanks.pop(t)
                for hf in range(2):
                    pt, pbf = vb[hf]
                    T.op(ACT, lambda: nc.scalar.activation(out=VRAW[:, sl, hf * 512:(hf + 1) * 512], in_=pt[:],
                                                           func=AF.Gelu_apprx_tanh,
                                                           accum_out=STATS[:, sl, hf:hf + 1]),
                         reads=[pbf], writes=[VRB[sl], STB[sl]])

            def stageB1s(t):
                sl = t % 2
                vs = t % 3
                T.op(ACT, lambda: nc.scalar.activation(out=VN[:, vs, :], in_=VRAW[:, sl, :], func=AF.Square,
                                                       accum_out=SQS[:, sl, 0:1]),
                     reads=[VRB[sl]], writes=[VNB[vs], SQB[sl]])

            def stageB2a(t):
                sl = t % 2
                st_ = lambda a, b: STATS[:, sl, a:b]
                T.op(DVE, lambda: nc.vector.tensor_tensor(out=st_(3, 4), in0=st_(0, 1), in1=st_(1, 2), op=ALU.add),
                     reads=[STB[sl]], writes=[STB[sl]])
                T.op(DVE, lambda: nc.vector.tensor_scalar(out=st_(4, 5), in0=st_(3, 4), scalar1=-1.0 / 1024,
                                                          scalar2=None, op0=ALU.mult),
                     reads=[STB[sl]], writes=[STB[sl]])
                T.op(DVE, lambda: nc.vector.tensor_scalar(out=st_(5, 6), in0=st_(4, 5), scalar1=st_(4, 5),
                                                          scalar2=-EPS, op0=ALU.mult, op1=ALU.add),
                     reads=[STB[sl]], writes=[STB[sl]])

            def stageB2b(t):
                sl = t % 2
                st_ = lambda a, b: STATS[:, sl, a:b]
                T.op(DVE, lambda: nc.vector.scalar_tensor_tensor(
                    out=RS[:, sl, 0:1], in0=SQS[:, sl, 0:1], scalar=1.0 / 1024, in1=st_(5, 6),
                    op0=ALU.mult, op1=ALU.subtract),
                    reads=[STB[sl], SQB[sl]], writes=[STB[sl]])
                T.op(POOL, lambda: nc.gpsimd.tensor_tensor(out=RS[:, sl, 0:1], in0=RS[:, sl, 0:1], in1=MHALF[:],
                                                           op=ALU.pow),
                     reads=[STB[sl], ONESB], writes=[STB[sl]])

            def stageB3a(t):
                sl = t % 2
                T.op(DVE, lambda: nc.vector.tensor_tensor(out=RS[:, sl, 1:2], in0=STATS[:, sl, 4:5],
                                                          in1=RS[:, sl, 0:1], op=ALU.mult),
                     reads=[STB[sl]], writes=[STB[sl]])

            def stageB3b(t):
                sl = t % 2
                vs = t % 3
                T.op(ACT, lambda: nc.scalar.activation(out=VRAW[:, sl, :], in_=VRAW[:, sl, :], func=AF.Identity,
                                                       scale=RS[:, sl, 0:1], bias=RS[:, sl, 1:2]),
                     reads=[VRB[sl], STB[sl]], writes=[VRB[sl]])
                T.op(DVE, lambda: nc.vector.tensor_tensor(out=VN[:, vs, :], in0=VRAW[:, sl, :], in1=LNG[:],
                                                          op=ALU.mult),
                     reads=[VRB[sl], LNGB], writes=[VNB[vs]])

            def stageC(t):
                vs = t % 3
                sp_ = [bank(), bank()]
                sbanks[t] = sp_
                for hh in range(8):
                    pt, pbf = sp_[hh // 4]
                    mm(pt[:, (hh % 4) * 128:(hh % 4 + 1) * 128], VN[:, vs, hh * 128:(hh + 1) * 128],
                       WST[:, hh * 128:(hh + 1) * 128], True, True, [VNB[vs], WSTB], pbf, hh % 4 == 3)

            def stageD(t):
                sp_ = sbanks.pop(t)
                for half in range(2):
                    pt, pbf = sp_[half]
                    tmp, tb = s32()
                    T.op(DVE, lambda: nc.vector.tensor_tensor(out=tmp, in0=pt[:],
                                                              in1=TT[:, half * 512:(half + 1) * 512], op=ALU.add),
                         reads=[pbf, TTB], writes=[tb])
                    uu = U[:, half * 4:(half + 1) * 4, t * 128:(t + 1) * 128]
                    ubs = [UB[half * 4 + i][t // 4] for i in range(4)]
                    T.op(POOL, lambda: nc.gpsimd.tensor_tensor(
                        out=uu, in0=tmp.rearrange("p (a b) -> p a b", a=4), in1=uu, op=ALU.mult),
                        reads=[tb] + ubs, writes=ubs)

            for i in range(16 + 2):
                if i < 16:
                    stageA(i)
                if i == 16:
                    PRE[("wb", 0)] = ring_load(wview(w_b[l])[:, :, 0:256], 8, 256)
                    PRE[("gb", 0)] = ring_load(wi[:, :, 3584:3584 + 256], 8, 256)
                if 0 <= i - 1 < 16:
                    stageB3a(i - 1)
                if i < 16:
                    stageB1(i)
                    stageB1s(i)
                    stageB2a(i)
                    stageB2b(i)
                if 0 <= i - 1 < 16:
                    stageB3b(i - 1)
                if 0 <= i - 2 < 16:
                    stageC(i - 2)
                    stageD(i - 2)

        def p8_merge(l):
            wi = wview(w_in[l])
            wb_ = wview(w_b[l])
            for m in range(8):
                if m % 2 == 0:
                    wb_v, wb_b = get_piece(("wb", m // 2), wb_[:, :, (m // 2) * 256:(m // 2 + 1) * 256], 8, 256)
                    c0 = 3584 + (m // 2) * 256
                    gb_v, gb_b = get_piece(("gb", m // 2), wi[:, :, c0:c0 + 256], 8, 256)
                for npair in ((0, 1), (2, 3)):
                    sig = {}

                    def ev_g(n, pt, pbf):
                        s_, sb_ = s32()
                        sig[n] = (s_, sb_)
                        T.op(ACT, lambda: nc.scalar.activation(out=s_, in_=pt[:], func=AF.Sigmoid),
                             reads=[pbf], writes=[sb_])
                    gemm([(gb_v[:, k, (m % 2) * 128:(m % 2 + 1) * 128], gb_b) for k in range(8)],
                         act_h, ev_g, npair)

                    def ev_y(n, pt, pbf, m=m):
                        s_, sb_ = sig[n]
                        T.op(DVE, lambda: nc.vector.tensor_tensor(out=s_, in0=pt[:], in1=s_, op=ALU.mult),
                             reads=[pbf, sb_], writes=[sb_])
                        T.op(DVE, lambda: nc.vector.tensor_tensor(out=MRG(m, n), in0=MRG(m, n), in1=s_, op=ALU.add),
                             reads=[sb_, MB[m][n]], writes=[MB[m][n]])
                    gemm([(wb_v[:, k, (m % 2) * 128:(m % 2 + 1) * 128], wb_b) for k in range(8)],
                         lambda k, n: (U[:, k, nsl(n)], UB[k][n]), ev_y, npair)

        def xadd_evac(m):
            def ev(n, pt, pbf):
                T.op(DVE, lambda: nc.vector.tensor_tensor(out=X[:, m, nsl(n)], in0=pt[:], in1=X[:, m, nsl(n)],
                                                          op=ALU.add),
                     reads=[pbf, XB[m][n]], writes=[XB[m][n]])
            return ev

        def p9_out(l):
            wo = wview(w_out[l])
            for npair in ((0, 1), (2, 3)):
                for m in range(8):
                    if m % 2 == 0:
                        wv, wbuf = ring_load(wo[:, :, (m // 2) * 256:(m // 2 + 1) * 256], 8, 256)
                    gemm([(wv[:, k, (m % 2) * 128:(m % 2 + 1) * 128], wbuf) for k in range(8)],
                         lambda k, n: (MRG(k, n), MB[k][n]), xadd_evac(m), npair)

        def p11_mlp(l, gcol):
            wu = wview(w_up[l])
            rms_n(gcol, False, 0)
            rms_n(gcol, False, 1)
            for q in range(4):
                for nset_u, fc in ([(ns_, fc_) for ns_ in ([0], [1], [2], [3]) for fc_ in range(8)] if q == 0
                                   else [(range(4), fc_) for fc_ in range(8)]):
                    if q == 0 and fc == 0 and 1 <= nset_u[0] <= 2:
                        rms_n(gcol, False, nset_u[0] + 1)
                    if fc % 2 == 0:
                        c0 = q * 1024 + (fc // 2) * 256
                        wv, wbuf = ring_load(wu[:, :, c0:c0 + 256], 8, 256)

                    def ev(n, pt, pbf, fc=fc):
                        r, rb = s32()
                        o = U[:, fc, nsl(n)]
                        if (fc + n) % 2 == 0:
                            T.op(ACT, lambda: nc.scalar.activation(out=r, in_=pt[:], func=AF.Relu),
                                 reads=[pbf], writes=[rb])
                            T.op(ACT, lambda: nc.scalar.activation(out=o, in_=r, func=AF.Square),
                                 reads=[rb], writes=[UB[fc][n]])
                        else:
                            T.op(DVE, lambda: nc.vector.tensor_scalar(out=r, in0=pt[:], scalar1=0.0, scalar2=None,
                                                                      op0=ALU.max),
                                 reads=[pbf], writes=[rb])
                            T.op(DVE, lambda: nc.vector.tensor_tensor(out=o, in0=r, in1=r, op=ALU.mult),
                                 reads=[rb], writes=[UB[fc][n]])
                    gemm([(wv[:, k, (fc % 2) * 128:(fc % 2 + 1) * 128], wbuf) for k in range(8)], act_h, ev, nset_u)
                wd = wview(w_down[l][q * 1024:(q + 1) * 1024, :])
                for nset in (((0, 1), (2, 3)) if q == 3 else (range(4),)):
                    for m in range(8):
                        if m % 2 == 0:
                            wv2, wbuf2 = ring_load(wd[:, :, (m // 2) * 256:(m // 2 + 1) * 256], 8, 256)
                        gemm([(wv2[:, k, (m % 2) * 128:(m % 2 + 1) * 128], wbuf2) for k in range(8)],
                             lambda k, n: (U[:, k, nsl(n)], UB[k][n]), xadd_evac(m), nset)

        def load_tables():
            handoff(allMB, [TABB])
            T.dma(SP, "tab",
                  [lambda: nc.sync.dma_start(out=MRGR[:, 0:8 * TK].rearrange("p (t k) -> p t k", t=8),
                                             in_=cf_d.rearrange("(t p) k -> p t k", p=128)),
                   lambda: nc.sync.dma_start(out=MRGR[:, 8 * TK:16 * TK].rearrange("p (t k) -> p t k", t=8),
                                             in_=sf_d.rearrange("(t p) k -> p t k", p=128))],
                  writes=[TABB])

        for b in range(nseq):
            xv = xT[b].rearrange("(c p) t -> p c t", p=128)
            for n in range(NT):
                T.dma(SP, f"xld{n}",
                      [(lambda c=c: nc.sync.dma_start(out=X[:, c, nsl(n)], in_=xv[:, c, nsl(n)])) for c in range(8)],
                      writes=[XB[c][n] for c in range(8)])
            for l in range(L):
                load_tables()
                for ph, fn_, args in (("p1_ain", p1_fourier_in, (l, l * 32 + 0)),
                                      ("p3_pq", p3_pq, ()), ("p4_dft", p4_dft, ()), ("p5_ma", p5_ma, (l,)),
                                      ("p6_u", p6_u, (l,)), ("p7_sgu", p7_sgu, (l,)), ("p8_mrg", p8_merge, (l,)),
                                      ("p9_out", p9_out, (l,)),
                                      ("p11_mlp", p11_mlp, (l, l * 32 + 8))):
                    T.phase = ph
                    fn_(*args)
            if final:
                T.phase = "rmsF"
                rmsnorm(L * 32, True)
            ov = outT[b].rearrange("(c p) t -> p c t", p=128)
            for n in range(NT):
                T.dma(SP, f"xst{n}",
                      [(lambda c=c: nc.sync.dma_start(out=ov[:, c, nsl(n)], in_=X[:, c, nsl(n)])) for c in range(8)],
                      reads=[XB[c][n] for c in range(8)])
        for n in range(NT):
            nc.sync.wait_ge(T.sems[f"xst{n}"], T.dcnt[f"xst{n}"])
    return nc


_CACHE = {}


def _get_nc(L, final):
    key = (L, final)
    if key not in _CACHE:
        _CACHE[key] = build(L, final)
    return _CACHE[key]


def _consts():
    s = np.arange(1024, dtype=np.float64)[:, None]
    k = np.arange(TK, dtype=np.float64)[None, :]
    ang = 2.0 * np.pi * ((s * k) % 2048.0) / 2048.0
    cf = np.cos(ang)
    sf = np.sin(ang)
    cf[:, 1025:] = 0.0
    sf[:, 1025:] = 0.0
    c = np.arange(64, dtype=np.float64)[:, None]
    m = np.arange(64, dtype=np.float64)[None, :]
    a64 = 2.0 * np.pi * ((c * m) % 64.0) / 64.0
    bd = np.zeros((128, 256), np.float64)
    for g in range(2):
        bd[g * 64:(g + 1) * 64, g * 64:(g + 1) * 64] = np.cos(a64)
        bd[g * 64:(g + 1) * 64, 128 + g * 64:128 + (g + 1) * 64] = np.sin(a64)
    alt = np.zeros((1, TK), np.float64)
    alt[0, :1025] = np.where(np.arange(1025) % 2 == 0, 1.0, -1.0)
    bf = ml_dtypes.bfloat16
    return {"cf": cf.astype(np.float32).astype(bf), "sf": sf.astype(np.float32).astype(bf),
            "bd": bd.astype(np.float32).astype(bf), "alt": alt.astype(np.float32).astype(bf)}


def _layer_inputs(inp, layers, with_final):
    L = len(layers)
    f32 = np.float32
    sel = lambda a: np.ascontiguousarray(np.asarray(a, f32)[layers])
    w_s = np.asarray(inp["w_s"], f32)[layers]
    wst = np.ascontiguousarray(w_s.transpose(0, 3, 1, 2).reshape(L, 128, 1024))
    lng = np.ascontiguousarray(np.broadcast_to(np.asarray(inp["ln_v_g"], f32)[layers][:, None, :], (L, 128, 1024)))
    bsr = np.ascontiguousarray(np.broadcast_to(
        np.asarray(inp["b_s"], f32)[layers].reshape(L, 1, 1024), (L, 128, 1024)))
    vecs = np.zeros((128, L * 32 + 8), f32)
    fm = lambda v: np.asarray(v, f32).reshape(8, 128).T
    for i, l in enumerate(layers):
        vecs[:, i * 32 + 0:i * 32 + 8] = fm(inp["g_mix"][l])
        vecs[:, i * 32 + 8:i * 32 + 16] = fm(inp["g_mlp"][l])
        vecs[:, i * 32 + 16:i * 32 + 24] = fm(inp["ln_v_b"][l])
        vecs[:, i * 32 + 24:i * 32 + 32] = fm(inp["ln_v_g"][l])
    vecs[:, L * 32:L * 32 + 8] = fm(inp["g_final"])
    d = {"w_in": sel(inp["w_in"]), "w_a": sel(inp["w_a"]), "w_b": sel(inp["w_b"]), "w_out": sel(inp["w_out"]),
         "w_up": sel(inp["w_up"]), "w_down": sel(inp["w_down"]), "wst": wst, "lng": lng, "bsr": bsr, "vecs": vecs}
    d.update(_consts())
    return d


def kernel(**inp):
    x = np.asarray(inp["x"], np.float32)
    xT = np.ascontiguousarray(x.transpose(0, 2, 1))
    if MODE == "fused":
        plan = [(list(range(DEPTH)), True)]
    else:
        plan = [([l], l == DEPTH - 1) for l in range(DEPTH)]
    cur = xT
    for layers, fin in plan:
        nc = _get_nc(len(layers), fin)
        shared = _layer_inputs(inp, layers, fin)
        in_maps = []
        for c in range(N_CORES):
            m = dict(shared)
            m["xT"] = np.ascontiguousarray(cur[c * NSEQ:(c + 1) * NSEQ])
            in_maps.append(m)
        res = run_bass_kernel_spmd(nc, in_maps, core_ids=list(range(N_CORES)))
        cur = np.concatenate([np.asarray(r["outT"], np.float32) for r in res.results], axis=0)
    return np.ascontiguousarray(cur.transpose(0, 2, 1))
```

```python
import math
from contextlib import ExitStack

import numpy as np
import ml_dtypes

import concourse.bass as bass
import concourse.mybir as mybir
from concourse.bass_utils import run_bass_kernel_spmd

F32 = mybir.dt.float32
BF16 = mybir.dt.bfloat16
AF = mybir.ActivationFunctionType
ALU = mybir.AluOpType

N_CORES = 8
DEPTH = 4
D = 1024
S = 2048
NSEQ = 4
NT = 4
TW = 512
RING_SLOTS = 5
RING_ELEMS = 2048
EPS = 1e-6
NORM = 1.0 / math.sqrt(2048.0 * 64.0)
TK = 1028

MODE = "fused"


class Buf:
    __slots__ = ("w", "r")

    def __init__(self):
        self.w = None
        self.r = {}


class Eng:
    def __init__(self, handle, key, is_pe=False):
        self.h = handle
        self.key = key
        self.cnt = 0
        self.seen = {}
        self.is_pe = is_pe


class Tracker:
    def __init__(self, nc, es):
        self.nc = nc
        self.es = es
        self.sems = {}
        self.dcnt = {}
        self.phase = "init"

    def sem(self, key):
        if key not in self.sems:
            self.sems[key] = self.es.enter_context(self.nc.semaphore(key))
            self.dcnt[key] = 0
        return self.sems[key]

    @staticmethod
    def _collect(reads, writes):
        deps = {}
        for b in reads:
            if b.w is not None:
                k, v = b.w
                if deps.get(k, 0) < v:
                    deps[k] = v
        for b in writes:
            if b.w is not None:
                k, v = b.w
                if deps.get(k, 0) < v:
                    deps[k] = v
            for k, v in b.r.items():
                if deps.get(k, 0) < v:
                    deps[k] = v
        return deps

    def _wait(self, E, deps):
        for k, v in deps.items():
            if E.is_pe and k == E.key:
                continue
            if E.seen.get(k, 0) >= v:
                continue
            E.h.wait_ge(self.sems[k], v)
            E.seen[k] = v

    @staticmethod
    def _mark(tag, reads, writes):
        k, v = tag
        for b in reads:
            if b.r.get(k, 0) < v:
                b.r[k] = v
        for b in writes:
            b.w = tag
            b.r = {}

    def op(self, E, fn, reads=(), writes=(), signal=True):
        self._wait(E, self._collect(reads, writes))
        inst = fn().annotate(self.phase)
        if signal:
            inst.then_inc(self.sem(E.key), 1)
            E.cnt += 1
            tag = (E.key, E.cnt)
        else:
            tag = (E.key, E.cnt + 1)
        self._mark(tag, reads, writes)

    def dma(self, Q, semkey, fns, reads=(), writes=()):
        sem = self.sem(semkey)
        self._wait(Q, self._collect(reads, writes))
        for fn in fns:
            fn().then_inc(sem, 16)
            self.dcnt[semkey] += 16
        self._mark((semkey, self.dcnt[semkey]), reads, writes)


def handoff(src, dst):
    merged = {}
    for b in src:
        if b.w is not None:
            k, v = b.w
            if merged.get(k, 0) < v:
                merged[k] = v
        for k, v in b.r.items():
            if merged.get(k, 0) < v:
                merged[k] = v
    for d in dst:
        d.w = None
        d.r = dict(merged)


def build(L, final, nseq=NSEQ):
    nc = bass.Bass("TRN2", target_bir_lowering=False, dynamic_dma_scratch_size=4096)

    def dram(name, shape, dt=F32, kind="ExternalInput"):
        return nc.dram_tensor(name, shape, dt, kind=kind).ap()

    xT = dram("xT", [nseq, D, S])
    outT = dram("outT", [nseq, D, S], kind="ExternalOutput")
    w_in = dram("w_in", [L, D, 4608])
    w_a = dram("w_a", [L, 512, D])
    w_b = dram("w_b", [L, D, D])
    w_out = dram("w_out", [L, D, D])
    w_up = dram("w_up", [L, D, 4096])
    w_down = dram("w_down", [L, 4096, D])
    wst_d = dram("wst", [L, 128, 1024])
    bsr_d = dram("bsr", [L, 128, 1024])
    lng_d = dram("lng", [L, 128, 1024])
    NV = L * 32 + 8
    vecs_d = dram("vecs", [128, NV])
    cf_d = dram("cf", [1024, TK], BF16)
    sf_d = dram("sf", [1024, TK], BF16)
    bd_d = dram("bd", [128, 256], BF16)
    alt_d = dram("alt", [1, TK], BF16)

    with ExitStack() as es:
        def sb(name, shape, dt):
            return es.enter_context(nc.sbuf_tensor(name, shape, dt))

        X = sb("X", [128, 8, S], F32)
        H = sb("H", [128, 8, S], BF16)
        U = sb("U", [128, 8, S], BF16)
        MRGR = sb("MRGR", [128, 2 * 8 * TK], BF16)
        RING = sb("RING", [128, RING_SLOTS, RING_ELEMS], BF16)
        S32 = sb("S32", [128, 4, TW], F32)
        S16 = sb("S16", [128, 2, TW], BF16)
        VRAW = sb("VRAW", [128, 2, 1024], F32)
        VN = sb("VN", [128, 3, 1024], BF16)
        TT = sb("TT", [128, 1024], F32)
        WST = sb("WST", [128, 1024], BF16)
        LNG = sb("LNG", [128, 1024], F32)
        VECS = sb("VECS", [128, NV], F32)
        BD = sb("BD", [128, 256], BF16)
        ONES = sb("ONES", [128, 128], BF16)
        ALT = sb("ALT", [1, TK], BF16)
        PHT = sb("PHT", [1, 512], BF16)
        STATS = sb("STATS", [128, 2, 8], F32)
        RS = sb("RS", [128, 2, 2], F32)
        MHALF = sb("MHALF", [128, 1], F32)
        SQS = sb("SQS", [128, 2, 2], F32)
        ps = [es.enter_context(nc.psum_tensor(f"ps{i}", [128, TW], F32)) for i in range(8)]

        T = Tracker(nc, es)

        PE = Eng(nc.tensor, "pe", is_pe=True)
        ACT = Eng(nc.scalar, "act")
        DVE = Eng(nc.vector, "dve")
        SP = Eng(nc.sync, "sp")
        POOL = Eng(nc.gpsimd, "pool")
        for e in (PE, ACT, DVE):
            T.sem(e.key)

        XB = [[Buf() for _ in range(NT)] for _ in range(8)]
        HB = [[Buf() for _ in range(NT)] for _ in range(8)]
        UB = [[Buf() for _ in range(NT)] for _ in range(8)]
        MB = [[Buf() for _ in range(NT)] for _ in range(8)]
        AINB = [[Buf() for _ in range(NT)] for _ in range(4)]
        AEB = [Buf() for _ in range(4)]
        AOB = [Buf() for _ in range(4)]
        PQB = [Buf() for _ in range(8)]
        FAB = [Buf() for _ in range(4)]
        TABB = Buf()
        PHB = Buf()
        CONSTB = Buf()
        ONESB = Buf()
        VRB = [Buf(), Buf()]
        VNB = [Buf(), Buf(), Buf()]
        STB = [Buf(), Buf()]
        SQB = [Buf(), Buf()]
        TTB = Buf()
        WSTB = Buf()
        LNGB = Buf()
        PB = [Buf() for _ in range(8)]
        S32B = [Buf() for _ in range(4)]
        S16B = [Buf() for _ in range(2)]
        SQAB = [Buf() for _ in range(8)]
        RINGB = [Buf() for _ in range(RING_SLOTS)]
        allUB = [b for r in UB for b in r]
        allMB = [b for r in MB for b in r]
        allXB = [b for r in XB for b in r]

        st = {"bank": 0, "s32": 0, "s16": 0, "ring": 0}

        def bank():
            i = st["bank"]
            st["bank"] = (i + 1) % 8
            return ps[i], PB[i]

        def s32():
            i = st["s32"]
            st["s32"] = (i + 1) % 4
            return S32[:, i, :], S32B[i]

        def s16():
            i = st["s16"]
            st["s16"] = (i + 1) % 10
            if i < 2:
                return S16[:, i, :], S16B[i]
            j = i - 2
            return VRAW[:, j // 4, :].bitcast(BF16)[:, (j % 4) * 512:(j % 4 + 1) * 512], SQAB[j]

        def ring_load(src, kc, cols):
            i = st["ring"]
            st["ring"] = (i + 1) % RING_SLOTS
            view = RING[:, i, 0:kc * cols].rearrange("p (k c) -> p k c", k=kc)
            T.dma(POOL, f"ring{i}", [lambda: nc.gpsimd.dma_start(out=view, in_=src)],
                  writes=[RINGB[i]])
            return view, RINGB[i]

        PRE = {}

        def get_piece(key, src, kc, cols):
            if key in PRE:
                return PRE.pop(key)
            return ring_load(src, kc, cols)

        def nsl(n):
            return slice(n * TW, (n + 1) * TW)

        def mm(out, lhsT, rhs, start, stop, reads, pbuf, signal):
            T.op(PE, lambda: nc.tensor.matmul(out, lhsT, rhs, start=start, stop=stop),
                 reads=reads, writes=[pbuf], signal=signal)

        def gemm(wl, act, evac, ns):
            bks = {n: bank() for n in ns}
            K = len(wl)
            for k in range(K):
                lh, lb = wl[k]
                for n in ns:
                    ra, rb = act(k, n)
                    pt, pbf = bks[n]
                    mm(pt[:], lh, ra, k == 0, k == K - 1, [lb, rb], pbf, k == K - 1)
            for n in ns:
                evac(n, *bks[n])

        def wview(w2d):
            return w2d.rearrange("(k p) m -> p k m", p=128)

        def AE(j):
            return U[:, 4 + j // 2, (j % 2) * 1024:(j % 2 + 1) * 1024]

        def AO(j):
            return U[:, 6 + j // 2, (j % 2) * 1024:(j % 2 + 1) * 1024]

        def PQ(t, which, j=None):
            base = (t % 2) * 1024 + which * 512
            if j is None:
                return U[:, t // 2, base:base + 512]
            return U[:, t // 2, base + j * 128:base + (j + 1) * 128]

        def CF(t):
            return MRGR[:, t * TK:(t + 1) * TK]

        def SF(t):
            return MRGR[:, 8 * TK + t * TK:8 * TK + (t + 1) * TK]

        def MRG(m, n):
            return MRGR[:, m * S + n * TW:m * S + (n + 1) * TW]

        T.dma(SP, "c0", [lambda: nc.sync.dma_start(out=VECS[:], in_=vecs_d),
                         lambda: nc.sync.dma_start(out=BD[:], in_=bd_d),
                         lambda: nc.sync.dma_start(out=ALT[:], in_=alt_d)],
              writes=[CONSTB])
        T.op(DVE, lambda: nc.vector.memset(ONES[:], 1.0), writes=[ONESB])
        T.op(DVE, lambda: nc.vector.memset(MHALF[:], -0.5), writes=[ONESB])

        def rmsnorm(gcol, inplace):
            for n in range(NT):
                rms_n(gcol, inplace, n)

        def rms_n(gcol, inplace, n):
            ph_save = T.phase
            T.phase = "rms"
            if True:
                pst, pbf = bank()
                for c in range(8):
                    sq, sqb = s16()
                    T.op(ACT, lambda: nc.scalar.activation(out=sq, in_=X[:, c, nsl(n)], func=AF.Square),
                         reads=[XB[c][n]], writes=[sqb])
                    mm(pst[:], ONES[:], sq, c == 0, c == 7, [sqb, ONESB], pbf, True)
                r, rb = s32()
                T.op(ACT, lambda: nc.scalar.activation(out=r, in_=pst[:], func=AF.Ln,
                                                       scale=1.0 / D, bias=EPS),
                     reads=[pbf], writes=[rb])
                T.op(ACT, lambda: nc.scalar.activation(out=r, in_=r, func=AF.Exp, scale=-0.5),
                     reads=[rb], writes=[rb])
                for c in range(8):
                    if inplace:
                        o, ob = X[:, c, nsl(n)], XB[c][n]
                    else:
                        o, ob = H[:, c, nsl(n)], HB[c][n]
                    T.op(DVE, lambda: nc.vector.scalar_tensor_tensor(
                        out=o, in0=X[:, c, nsl(n)], scalar=VECS[:, gcol + c:gcol + c + 1], in1=r,
                        op0=ALU.mult, op1=ALU.mult),
                        reads=[XB[c][n], rb, CONSTB], writes=[ob])
            T.phase = ph_save

        def act_h(k, n):
            return H[:, k, nsl(n)], HB[k][n]

        def p1_fourier_in(l, gcol, pre):
            wi = wview(w_in[l])
            handoff(allUB, [b for r in AINB for b in r] + AEB + AOB)
            pcs = [ring_load(wi[:, :, jp * 256:(jp + 1) * 256], 8, 256) for jp in range(2)]
            if pre:
                rms_n(gcol, False, 2)
            else:
                rms_n(gcol, False, 0)
                rms_n(gcol, False, 1)
            for n in range(NT):
                if pre and n == 1:
                    rms_n(gcol, False, 3)
                if (not pre) and n >= 1 and n + 1 < NT:
                    rms_n(gcol, False, n + 1)
                for j in range(4):
                    wv, wbuf = pcs[j // 2]
                    jj = j % 2

                    def ev(n_, pt, pbf, j=j):
                        if j % 2 == 0:
                            T.op(ACT, lambda: nc.scalar.copy(out=U[:, j, nsl(n_)], in_=pt[:]),
                                 reads=[pbf], writes=[AINB[j][n_]])
                        else:
                            T.op(DVE, lambda: nc.vector.tensor_copy(out=U[:, j, nsl(n_)], in_=pt[:]),
                                 reads=[pbf], writes=[AINB[j][n_]])
                    gemm([(wv[:, k, jj * 128:(jj + 1) * 128], wbuf) for k in range(8)], act_h, ev, [n])
            for j in range(4):
                a = U[:, j, :]
                T.op(DVE, lambda: nc.vector.tensor_tensor(out=AE(j)[:, 1:1024], in0=a[:, 1:1024],
                                                          in1=a[:, 2047:1024:-1], op=ALU.add),
                     reads=AINB[j], writes=[AEB[j]])
                T.op(DVE, lambda: nc.vector.tensor_tensor(out=AO(j)[:, 1:1024], in0=a[:, 1:1024],
                                                          in1=a[:, 2047:1024:-1], op=ALU.subtract),
                     reads=AINB[j], writes=[AOB[j]])
                T.op(DVE, lambda: nc.vector.tensor_copy(out=AE(j)[:, 0:1], in_=a[:, 0:1]),
                     reads=AINB[j], writes=[AEB[j]])
                T.op(DVE, lambda: nc.vector.tensor_copy(out=AO(j)[:, 0:1], in_=a[:, 0:1]),
                     reads=AINB[j], writes=[AOB[j]])
            pt, pbf = bank()
            for j in range(4):
                mm(pt[0:1, j * 128:(j + 1) * 128], U[:, j, 1024:1025], BD[:, 0:128], True, True,
                   [AINB[j][2], CONSTB], pbf, j == 3)
            T.op(ACT, lambda: nc.scalar.copy(out=PHT[0:1, :], in_=pt[0:1, :]), reads=[pbf], writes=[PHB])

        def p3_pq():
            handoff([b for r in AINB for b in r], PQB)
            for t in range(8):
                pp, ppb = bank()
                pq, pqb = bank()
                for j in range(4):
                    mm(pp[:, j * 128:(j + 1) * 128], AE(j)[:, t * 128:(t + 1) * 128], BD[:, 0:128],
                       True, True, [AEB[j], CONSTB], ppb, j == 3)
                for j in range(4):
                    mm(pq[:, j * 128:(j + 1) * 128], AO(j)[:, t * 128:(t + 1) * 128], BD[:, 128:256],
                       True, True, [AOB[j], CONSTB], pqb, j == 3)
                T.op(ACT, lambda: nc.scalar.copy(out=PQ(t, 0), in_=pp[:]), reads=[ppb], writes=[PQB[t]])
                T.op(DVE, lambda: nc.vector.tensor_copy(out=PQ(t, 1), in_=pq[:]), reads=[pqb], writes=[PQB[t]])

        def p4_dft():
            handoff(AEB + AOB, FAB)
            for j in range(4):
                fa = U[:, 4 + j, :]
                for q in range(2):
                    pe_, peb = bank()
                    po, pob = bank()
                    ks = slice(q * 512, (q + 1) * 512)
                    for t in range(8):
                        mm(pe_[:], PQ(t, 0, j), CF(t)[:, ks], t == 0, False, [PQB[t], TABB], peb, False)
                    mm(pe_[:], PHT[0:1, j * 128:(j + 1) * 128], ALT[0:1, ks], False, True,
                       [PHB, CONSTB], peb, True)
                    for t in range(8):
                        mm(po[:], PQ(t, 1, j), SF(t)[:, ks], t == 0, t == 7, [PQB[t], TABB], pob, t == 7)
                    osb, ob = s32()
                    T.op(ACT, lambda: nc.scalar.activation(out=osb, in_=po[:], func=AF.Copy, scale=NORM),
                         reads=[pob], writes=[ob])
                    T.op(DVE, lambda: nc.vector.scalar_tensor_tensor(
                        out=fa[:, ks], in0=pe_[:], scalar=NORM, in1=osb, op0=ALU.mult, op1=ALU.subtract),
                        reads=[peb, ob], writes=[FAB[j]])
                    if q == 0:
                        o2, i0, i1 = fa[:, 2047:1536:-1], pe_[:, 1:512], osb[:, 1:512]
                    else:
                        o2, i0, i1 = fa[:, 1536:1024:-1], pe_[:, 0:512], osb[:, 0:512]
                    T.op(DVE, lambda: nc.vector.scalar_tensor_tensor(
                        out=o2, in0=i0, scalar=NORM, in1=i1, op0=ALU.mult, op1=ALU.add),
                        reads=[peb, ob], writes=[FAB[j]])
            pt, pbf = bank()
            for j in range(4):
                for t in range(8):
                    mm(pt[:, j:j + 1], PQ(t, 0, j), CF(t)[:, 1024:1025], t == 0, False, [PQB[t], TABB], pbf, False)
                mm(pt[:, j:j + 1], PHT[0:1, j * 128:(j + 1) * 128], ALT[0:1, 1024:1025], False, True,
                   [PHB, CONSTB], pbf, True)
            T.op(DVE, lambda: nc.vector.tensor_scalar(out=U[:, 4:8, 1024], in0=pt[:, 0:4], scalar1=NORM,
                                                      scalar2=None, op0=ALU.mult),
                 reads=[pbf], writes=FAB)

        def p5_ma(l):
            wi = wview(w_in[l])
            wa = wview(w_a[l])
            handoff([TABB], allMB)
            for m in range(8):
                if m % 4 == 0:
                    wa_v, wa_b = ring_load(wa[:, :, (m // 4) * 512:(m // 4 + 1) * 512], 4, 512)
                if m % 2 == 0:
                    c0 = 2560 + (m // 2) * 256
                    ga_v, ga_b = ring_load(wi[:, :, c0:c0 + 256], 8, 256)
                for npair in ((0, 1), (2, 3)):
                    sig = {}

                    def ev_g(n, pt, pbf):
                        s_, sb_ = s32()
                        sig[n] = (s_, sb_)
                        T.op(ACT, lambda: nc.scalar.activation(out=s_, in_=pt[:], func=AF.Sigmoid),
                             reads=[pbf], writes=[sb_])
                    gemm([(ga_v[:, k, (m % 2) * 128:(m % 2 + 1) * 128], ga_b) for k in range(8)],
                         act_h, ev_g, npair)

                    def ev_y(n, pt, pbf, m=m):
                        s_, sb_ = sig[n]
                        T.op(DVE, lambda: nc.vector.tensor_tensor(out=MRG(m, n), in0=pt[:], in1=s_, op=ALU.mult),
                             reads=[pbf, sb_], writes=[MB[m][n]])
                    gemm([(wa_v[:, j, (m % 4) * 128:(m % 4 + 1) * 128], wa_b) for j in range(4)],
                         lambda j, n: (U[:, 4 + j, nsl(n)], FAB[j]), ev_y, npair)

        def p6_u(l):
            wi = wview(w_in[l])
            handoff(PQB + FAB, allUB)
            for m in range(8):
                if m % 2 == 0:
                    c0 = 512 + (m // 2) * 256
                    wv, wbuf = ring_load(wi[:, :, c0:c0 + 256], 8, 256)

                def ev(n, pt, pbf, m=m):
                    T.op(ACT, lambda: nc.scalar.activation(out=U[:, m, nsl(n)], in_=pt[:], func=AF.Gelu_apprx_tanh),
                         reads=[pbf], writes=[UB[m][n]])
                gemm([(wv[:, k, (m % 2) * 128:(m % 2 + 1) * 128], wbuf) for k in range(8)], act_h, ev, range(4))

        def p7_sgu(l):
            wi = wview(w_in[l])
            lnb_col = l * 32 + 16
            lng_col = l * 32 + 24
            handoff(SQAB[0:4], [VRB[0]])
            handoff(SQAB[4:8], [VRB[1]])
            T.dma(POOL, "wst", [lambda: nc.gpsimd.dma_start(out=WST[:], in_=wst_d[l])], writes=[WSTB])
            T.dma(SP, "bsr", [lambda: nc.sync.dma_start(out=VRAW[:, 0, :], in_=bsr_d[l])], writes=[VRB[0]])
            T.dma(SP, "lng", [lambda: nc.sync.dma_start(out=LNG[:], in_=lng_d[l])], writes=[LNGB])
            for half in range(2):
                pt, pbf = bank()
                for hq in range(4):
                    hh = half * 4 + hq
                    mm(pt[:, hq * 128:(hq + 1) * 128], ONES[:], WST[:, hh * 128:(hh + 1) * 128], True, True,
                       [ONESB, WSTB], pbf, hq == 3)
                for hq in range(4):
                    hh = half * 4 + hq
                    T.op(DVE, lambda: nc.vector.scalar_tensor_tensor(
                        out=TT[:, hh * 128:(hh + 1) * 128], in0=pt[:, hq * 128:(hq + 1) * 128],
                        scalar=VECS[:, lnb_col + hh:lnb_col + hh + 1], in1=VRAW[:, 0, hh * 128:(hh + 1) * 128],
                        op0=ALU.mult, op1=ALU.add),
                        reads=[pbf, VRB[0], CONSTB], writes=[TTB])
            wv = []
            for c2 in range(4):
                c0 = 1536 + c2 * 256
                wv.append(ring_load(wi[:, :, c0:c0 + 256], 8, 256))

            vbanks = {}
            sbanks = {}

            def stageA(t):
                vb = [bank(), bank()]
                vbanks[t] = vb
                for c2 in range(4):
                    pt, pbf = vb[c2 // 2]
                    for k in range(8):
                        mm(pt[:, (c2 % 2) * 256:(c2 % 2 + 1) * 256], H[:, k, t * 128:(t + 1) * 128],
                           wv[c2][0][:, k, :], k == 0, k == 7, [HB[k][t // 4], wv[c2][1]], pbf,
                           k == 7 and c2 % 2 == 1)

            def stageB1(t):
                sl = t % 2
                vs = t % 3
                vb = vbanks.pop(t)
                for hf in range(2):
                    pt, pbf = vb[hf]
                    T.op(ACT, lambda: nc.scalar.activation(out=VRAW[:, sl, hf * 512:(hf + 1) * 512], in_=pt[:],
                                                           func=AF.Gelu_apprx_tanh,
                                                           accum_out=STATS[:, sl, hf:hf + 1]),
                         reads=[pbf], writes=[VRB[sl], STB[sl]])

            def stageB1s(t):
                sl = t % 2
                vs = t % 3
                T.op(ACT, lambda: nc.scalar.activation(out=VN[:, vs, :], in_=VRAW[:, sl, :], func=AF.Square,
                                                       accum_out=SQS[:, sl, 0:1]),
                     reads=[VRB[sl]], writes=[VNB[vs], SQB[sl]])

            def stageB2a(t):
                sl = t % 2
                st_ = lambda a, b: STATS[:, sl, a:b]
                T.op(DVE, lambda: nc.vector.tensor_tensor(out=st_(3, 4), in0=st_(0, 1), in1=st_(1, 2), op=ALU.add),
                     reads=[STB[sl]], writes=[STB[sl]])
                T.op(DVE, lambda: nc.vector.tensor_scalar(out=st_(4, 5), in0=st_(3, 4), scalar1=-1.0 / 1024,
                                                          scalar2=None, op0=ALU.mult),
                     reads=[STB[sl]], writes=[STB[sl]])
                T.op(DVE, lambda: nc.vector.tensor_scalar(out=st_(5, 6), in0=st_(4, 5), scalar1=st_(4, 5),
                                                          scalar2=-EPS, op0=ALU.mult, op1=ALU.add),
                     reads=[STB[sl]], writes=[STB[sl]])

            def stageB2b(t):
                sl = t % 2
                st_ = lambda a, b: STATS[:, sl, a:b]
                T.op(DVE, lambda: nc.vector.scalar_tensor_tensor(
                    out=RS[:, sl, 0:1], in0=SQS[:, sl, 0:1], scalar=1.0 / 1024, in1=st_(5, 6),
                    op0=ALU.mult, op1=ALU.subtract),
                    reads=[STB[sl], SQB[sl]], writes=[STB[sl]])
                T.op(POOL, lambda: nc.gpsimd.tensor_tensor(out=RS[:, sl, 0:1], in0=RS[:, sl, 0:1], in1=MHALF[:],
                                                           op=ALU.pow),
                     reads=[STB[sl], ONESB], writes=[STB[sl]])

            def stageB3a(t):
                sl = t % 2
                T.op(DVE, lambda: nc.vector.tensor_tensor(out=RS[:, sl, 1:2], in0=STATS[:, sl, 4:5],
                                                          in1=RS[:, sl, 0:1], op=ALU.mult),
                     reads=[STB[sl]], writes=[STB[sl]])

            def stageB3b(t):
                sl = t % 2
                vs = t % 3
                T.op(ACT, lambda: nc.scalar.activation(out=VRAW[:, sl, :], in_=VRAW[:, sl, :], func=AF.Identity,
                                                       scale=RS[:, sl, 0:1], bias=RS[:, sl, 1:2]),
                     reads=[VRB[sl], STB[sl]], writes=[VRB[sl]])
                T.op(DVE, lambda: nc.vector.tensor_tensor(out=VN[:, vs, :], in0=VRAW[:, sl, :], in1=LNG[:],
                                                          op=ALU.mult),
                     reads=[VRB[sl], LNGB], writes=[VNB[vs]])

            def stageC(t):
                vs = t % 3
                sp_ = [bank(), bank()]
                sbanks[t] = sp_
                for hh in range(8):
                    pt, pbf = sp_[hh // 4]
                    mm(pt[:, (hh % 4) * 128:(hh % 4 + 1) * 128], VN[:, vs, hh * 128:(hh + 1) * 128],
                       WST[:, hh * 128:(hh + 1) * 128], True, True, [VNB[vs], WSTB], pbf, hh % 4 == 3)

            def stageD(t):
                sp_ = sbanks.pop(t)
                for half in range(2):
                    pt, pbf = sp_[half]
                    tmp, tb = s32()
                    T.op(DVE, lambda: nc.vector.tensor_tensor(out=tmp, in0=pt[:],
                                                              in1=TT[:, half * 512:(half + 1) * 512], op=ALU.add),
                         reads=[pbf, TTB], writes=[tb])
                    uu = U[:, half * 4:(half + 1) * 4, t * 128:(t + 1) * 128]
                    ubs = [UB[half * 4 + i][t // 4] for i in range(4)]
                    T.op(POOL, lambda: nc.gpsimd.tensor_tensor(
                        out=uu, in0=tmp.rearrange("p (a b) -> p a b", a=4), in1=uu, op=ALU.mult),
                        reads=[tb] + ubs, writes=ubs)

            for i in range(16 + 2):
                if i < 16:
                    stageA(i)
                if i == 16:
                    PRE[("wb", 0)] = ring_load(wview(w_b[l])[:, :, 0:256], 8, 256)
                    PRE[("gb", 0)] = ring_load(wi[:, :, 3584:3584 + 256], 8, 256)
                if 0 <= i - 1 < 16:
                    stageB3a(i - 1)
                if i < 16:
                    stageB1(i)
                    stageB1s(i)
                    stageB2a(i)
                    stageB2b(i)
                if 0 <= i - 1 < 16:
                    stageB3b(i - 1)
                if 0 <= i - 2 < 16:
                    stageC(i - 2)
                    stageD(i - 2)
            handoff([VRB[0]], SQAB[0:4])
            handoff([VRB[1]], SQAB[4:8])

        def p8_merge(l):
            wi = wview(w_in[l])
            wb_ = wview(w_b[l])
            for m in range(8):
                if m % 2 == 0:
                    wb_v, wb_b = get_piece(("wb", m // 2), wb_[:, :, (m // 2) * 256:(m // 2 + 1) * 256], 8, 256)
                    c0 = 3584 + (m // 2) * 256
                    gb_v, gb_b = get_piece(("gb", m // 2), wi[:, :, c0:c0 + 256], 8, 256)
                for npair in ((0, 1), (2, 3)):
                    sig = {}

                    def ev_g(n, pt, pbf):
                        s_, sb_ = s32()
                        sig[n] = (s_, sb_)
                        T.op(ACT, lambda: nc.scalar.activation(out=s_, in_=pt[:], func=AF.Sigmoid),
                             reads=[pbf], writes=[sb_])
                    gemm([(gb_v[:, k, (m % 2) * 128:(m % 2 + 1) * 128], gb_b) for k in range(8)],
                         act_h, ev_g, npair)

                    def ev_y(n, pt, pbf, m=m):
                        s_, sb_ = sig[n]
                        T.op(DVE, lambda: nc.vector.tensor_tensor(out=s_, in0=pt[:], in1=s_, op=ALU.mult),
                             reads=[pbf, sb_], writes=[sb_])
                        T.op(DVE, lambda: nc.vector.tensor_tensor(out=MRG(m, n), in0=MRG(m, n), in1=s_, op=ALU.add),
                             reads=[sb_, MB[m][n]], writes=[MB[m][n]])
                    gemm([(wb_v[:, k, (m % 2) * 128:(m % 2 + 1) * 128], wb_b) for k in range(8)],
                         lambda k, n: (U[:, k, nsl(n)], UB[k][n]), ev_y, npair)

        def xadd_evac(m):
            def ev(n, pt, pbf):
                T.op(DVE, lambda: nc.vector.tensor_tensor(out=X[:, m, nsl(n)], in0=pt[:], in1=X[:, m, nsl(n)],
                                                          op=ALU.add),
                     reads=[pbf, XB[m][n]], writes=[XB[m][n]])
            return ev

        def p9_out(l, hook):
            wo = wview(w_out[l])
            for npair in ((0, 1), (2, 3)):
                for m in range(8):
                    if m % 2 == 0:
                        wv, wbuf = ring_load(wo[:, :, (m // 2) * 256:(m // 2 + 1) * 256], 8, 256)
                    gemm([(wv[:, k, (m % 2) * 128:(m % 2 + 1) * 128], wbuf) for k in range(8)],
                         lambda k, n: (MRG(k, n), MB[k][n]), xadd_evac(m), npair)
                    if npair[0] == 2:
                        hook(m)

        def p11_mlp(l, gcol, hook):
            wu = wview(w_up[l])
            rms_n(gcol, False, 2)
            for q in range(4):
                for nset_u, fc in ([(ns_, fc_) for ns_ in ([0], [1], [2], [3]) for fc_ in range(8)] if q == 0
                                   else [(range(4), fc_) for fc_ in range(8)]):
                    if q == 0 and fc == 0 and nset_u[0] == 1:
                        rms_n(gcol, False, 3)
                    if fc % 2 == 0:
                        c0 = q * 1024 + (fc // 2) * 256
                        wv, wbuf = ring_load(wu[:, :, c0:c0 + 256], 8, 256)

                    def ev(n, pt, pbf, fc=fc):
                        r, rb = s32()
                        o = U[:, fc, nsl(n)]
                        if (fc + n) % 2 == 0:
                            T.op(ACT, lambda: nc.scalar.activation(out=r, in_=pt[:], func=AF.Relu),
                                 reads=[pbf], writes=[rb])
                            T.op(ACT, lambda: nc.scalar.activation(out=o, in_=r, func=AF.Square),
                                 reads=[rb], writes=[UB[fc][n]])
                        else:
                            T.op(DVE, lambda: nc.vector.tensor_scalar(out=r, in0=pt[:], scalar1=0.0, scalar2=None,
                                                                      op0=ALU.max),
                                 reads=[pbf], writes=[rb])
                            T.op(DVE, lambda: nc.vector.tensor_tensor(out=o, in0=r, in1=r, op=ALU.mult),
                                 reads=[rb], writes=[UB[fc][n]])
                    gemm([(wv[:, k, (fc % 2) * 128:(fc % 2 + 1) * 128], wbuf) for k in range(8)], act_h, ev, nset_u)
                wd = wview(w_down[l][q * 1024:(q + 1) * 1024, :])
                for nset in (((0, 1), (2, 3)) if q == 3 else (range(4),)):
                    for m in range(8):
                        if m % 2 == 0:
                            wv2, wbuf2 = ring_load(wd[:, :, (m // 2) * 256:(m // 2 + 1) * 256], 8, 256)
                        gemm([(wv2[:, k, (m % 2) * 128:(m % 2 + 1) * 128], wbuf2) for k in range(8)],
                             lambda k, n: (U[:, k, nsl(n)], UB[k][n]), xadd_evac(m), nset)
                        if q == 3 and nset[0] == 2:
                            hook(m)

        def load_tables():
            handoff(allMB, [TABB])
            T.dma(SP, "tab",
                  [lambda: nc.sync.dma_start(out=MRGR[:, 0:8 * TK].rearrange("p (t k) -> p t k", t=8),
                                             in_=cf_d.rearrange("(t p) k -> p t k", p=128)),
                   lambda: nc.sync.dma_start(out=MRGR[:, 8 * TK:16 * TK].rearrange("p (t k) -> p t k", t=8),
                                             in_=sf_d.rearrange("(t p) k -> p t k", p=128))],
                  writes=[TABB])

        for b in range(nseq):
            xv = xT[b].rearrange("(c p) t -> p c t", p=128)
            for n in range(NT):
                T.dma(SP, f"xld{n}",
                      [(lambda c=c: nc.sync.dma_start(out=X[:, c, nsl(n)], in_=xv[:, c, nsl(n)])) for c in range(8)],
                      writes=[XB[c][n] for c in range(8)])
            for l in range(L):
                load_tables()

                def hook_mlp(m, l=l):
                    if m == 1:
                        rms_n(l * 32 + 8, False, 0)
                    elif m == 3:
                        rms_n(l * 32 + 8, False, 1)

                def hook_next(m, l=l):
                    if l + 1 < L:
                        args = ((l + 1) * 32 + 0, False)
                    elif final:
                        args = (L * 32, True)
                    else:
                        return
                    if m == 1:
                        rms_n(args[0], args[1], 0)
                    elif m == 3:
                        rms_n(args[0], args[1], 1)

                for ph, fn_, args in (("p1_ain", p1_fourier_in, (l, l * 32 + 0, l > 0)),
                                      ("p3_pq", p3_pq, ()), ("p4_dft", p4_dft, ()), ("p5_ma", p5_ma, (l,)),
                                      ("p6_u", p6_u, (l,)), ("p7_sgu", p7_sgu, (l,)), ("p8_mrg", p8_merge, (l,)),
                                      ("p9_out", p9_out, (l, hook_mlp)),
                                      ("p11_mlp", p11_mlp, (l, l * 32 + 8, hook_next))):
                    T.phase = ph
                    fn_(*args)
            if final:
                T.phase = "rmsF"
                rms_n(L * 32, True, 2)
                rms_n(L * 32, True, 3)
            ov = outT[b].rearrange("(c p) t -> p c t", p=128)
            for n in range(NT):
                T.dma(SP, f"xst{n}",
                      [(lambda c=c: nc.sync.dma_start(out=ov[:, c, nsl(n)], in_=X[:, c, nsl(n)])) for c in range(8)],
                      reads=[XB[c][n] for c in range(8)])
        for n in range(NT):
            nc.sync.wait_ge(T.sems[f"xst{n}"], T.dcnt[f"xst{n}"])
    return nc


_CACHE = {}


def _get_nc(L, final):
    key = (L, final)
    if key not in _CACHE:
        _CACHE[key] = build(L, final)
    return _CACHE[key]


def _consts():
    s = np.arange(1024, dtype=np.float64)[:, None]
    k = np.arange(TK, dtype=np.float64)[None, :]
    ang = 2.0 * np.pi * ((s * k) % 2048.0) / 2048.0
    cf = np.cos(ang)
    sf = np.sin(ang)
    cf[:, 1025:] = 0.0
    sf[:, 1025:] = 0.0
    c = np.arange(64, dtype=np.float64)[:, None]
    m = np.arange(64, dtype=np.float64)[None, :]
    a64 = 2.0 * np.pi * ((c * m) % 64.0) / 64.0
    bd = np.zeros((128, 256), np.float64)
    for g in range(2):
        bd[g * 64:(g + 1) * 64, g * 64:(g + 1) * 64] = np.cos(a64)
        bd[g * 64:(g + 1) * 64, 128 + g * 64:128 + (g + 1) * 64] = np.sin(a64)
    alt = np.zeros((1, TK), np.float64)
    alt[0, :1025] = np.where(np.arange(1025) % 2 == 0, 1.0, -1.0)
    bf = ml_dtypes.bfloat16
    return {"cf": cf.astype(np.float32).astype(bf), "sf": sf.astype(np.float32).astype(bf),
            "bd": bd.astype(np.float32).astype(bf), "alt": alt.astype(np.float32).astype(bf)}


def _layer_inputs(inp, layers, with_final):
    L = len(layers)
    f32 = np.float32
    sel = lambda a: np.ascontiguousarray(np.asarray(a, f32)[layers])
    w_s = np.asarray(inp["w_s"], f32)[layers]
    wst = np.ascontiguousarray(w_s.transpose(0, 3, 1, 2).reshape(L, 128, 1024))
    lng = np.ascontiguousarray(np.broadcast_to(np.asarray(inp["ln_v_g"], f32)[layers][:, None, :], (L, 128, 1024)))
    bsr = np.ascontiguousarray(np.broadcast_to(
        np.asarray(inp["b_s"], f32)[layers].reshape(L, 1, 1024), (L, 128, 1024)))
    vecs = np.zeros((128, L * 32 + 8), f32)
    fm = lambda v: np.asarray(v, f32).reshape(8, 128).T
    for i, l in enumerate(layers):
        vecs[:, i * 32 + 0:i * 32 + 8] = fm(inp["g_mix"][l])
        vecs[:, i * 32 + 8:i * 32 + 16] = fm(inp["g_mlp"][l])
        vecs[:, i * 32 + 16:i * 32 + 24] = fm(inp["ln_v_b"][l])
        vecs[:, i * 32 + 24:i * 32 + 32] = fm(inp["ln_v_g"][l])
    vecs[:, L * 32:L * 32 + 8] = fm(inp["g_final"])
    d = {"w_in": sel(inp["w_in"]), "w_a": sel(inp["w_a"]), "w_b": sel(inp["w_b"]), "w_out": sel(inp["w_out"]),
         "w_up": sel(inp["w_up"]), "w_down": sel(inp["w_down"]), "wst": wst, "lng": lng, "bsr": bsr, "vecs": vecs}
    d.update(_consts())
    return d


def kernel(**inp):
    x = np.asarray(inp["x"], np.float32)
    xT = np.ascontiguousarray(x.transpose(0, 2, 1))
    if MODE == "fused":
        plan = [(list(range(DEPTH)), True)]
    else:
        plan = [([l], l == DEPTH - 1) for l in range(DEPTH)]
    cur = xT
    for layers, fin in plan:
        nc = _get_nc(len(layers), fin)
        shared = _layer_inputs(inp, layers, fin)
        in_maps = []
        for c in range(N_CORES):
            m = dict(shared)
            m["xT"] = np.ascontiguousarray(cur[c * NSEQ:(c + 1) * NSEQ])
            in_maps.append(m)
        res = run_bass_kernel_spmd(nc, in_maps, core_ids=list(range(N_CORES)))
        cur = np.concatenate([np.asarray(r["outT"], np.float32) for r in res.results], axis=0)
    return np.ascontiguousarray(cur.transpose(0, 2, 1))
```
